# Optimizing a Trainium2 kernel written in Bass

```python
import jax
import jax.numpy as jnp
from jax import lax
import numpy as np

D_MODEL = 2048
BATCH = 2
SEQ = 16384
DEPTH = 4

GRID_W = 64
CTX_LEN = 256
N_MIXERS = 3
N_POOL_LAYERS = (DEPTH + 2) // 3
N_NA_LAYERS = (DEPTH + 1) // 3
N_CONV_LAYERS = DEPTH // 3
POOL_WINDOWS = (2, 4, 8, 16)
N_POOL_GROUPS = 4
POOL_GROUP = D_MODEL // N_POOL_GROUPS
NA_HEAD_DIM = 64
NA_HEADS = D_MODEL // NA_HEAD_DIM
WIN_H = 8
WIN_W = 16
CONV_WIDTH = 31
FFN_CONV_WIDTH = 3
D_FF = ((8 * D_MODEL // 3 + 255) // 256) * 256
N_MOD = 6
NORM_EPS = 1e-6
LN_EPS = 1e-5
NEG_INF = -1e30

kernel_name = 'hybrid_pool_natten_conformer_dit'


def _rms_norm(x):
    xf = x.astype(jnp.float32)
    y = xf * lax.rsqrt(jnp.mean(xf * xf, axis=-1, keepdims=True) + NORM_EPS)
    return y.astype(x.dtype)


def _layer_norm(x, g, b):
    xf = x.astype(jnp.float32)
    mu = jnp.mean(xf, axis=-1, keepdims=True)
    var = jnp.mean(jnp.square(xf - mu), axis=-1, keepdims=True)
    return ((xf - mu) * lax.rsqrt(var + LN_EPS)).astype(x.dtype) * g + b


def _modulation(cvec, w, b):
    m = jax.nn.silu(cvec) @ w + b
    return m.reshape(cvec.shape[0], N_MOD, 1, cvec.shape[-1])


def _ada_in(x, m, k):
    return _rms_norm(x) * (1 + m[:, k + 1]) + m[:, k]


def _depthwise_conv(u, w, b):
    k = w.shape[0]
    y = lax.conv_general_dilated(u, w[:, None, :], window_strides=(1,),
                                 padding=[(k // 2, k - 1 - k // 2)],
                                 dimension_numbers=('NWC', 'WIO', 'NWC'),
                                 feature_group_count=u.shape[-1])
    return y + b


def pool_mixer(h, w_grp, ls):
    B, L, D = h.shape
    hf = h.astype(jnp.float32)
    csum = jnp.concatenate([jnp.zeros((B, 1, D), jnp.float32), jnp.cumsum(hf, axis=1)], axis=1)
    t = jnp.arange(L)
    pooled = []
    for g, w in enumerate(POOL_WINDOWS):
        lo = jnp.clip(t - w // 2, 0, L)
        hi = jnp.clip(t - w // 2 + w, 0, L)
        cg = csum[..., g * POOL_GROUP:(g + 1) * POOL_GROUP]
        s = jnp.take(cg, hi, axis=1) - jnp.take(cg, lo, axis=1)
        pooled.append(s / (hi - lo).astype(jnp.float32)[None, :, None])
    pooled = jnp.stack(pooled, axis=2)
    d = (pooled - hf.reshape(B, L, N_POOL_GROUPS, POOL_GROUP)).astype(h.dtype)
    y = jnp.einsum('blgc,gce->blge', d, w_grp).reshape(B, L, D)
    return y * ls


def neighbourhood_attention(q, k, v, kc, vc, rpb):
    B, N, H, Dh = q.shape
    rows = N // GRID_W
    win_h = min(WIN_H, rows)
    scale = Dh ** -0.5
    qg = q.reshape(B, rows, GRID_W, H, Dh)
    kg = k.reshape(B, rows, GRID_W, H, Dh)
    vg = v.reshape(B, rows, GRID_W, H, Dh)
    col = jnp.arange(GRID_W)
    col_start = jnp.clip(col - WIN_W // 2, 0, GRID_W - WIN_W)
    col_valid = (col[None, :] >= col_start[:, None]) & (col[None, :] < col_start[:, None] + WIN_W)
    col_off = jnp.clip(col[None, :] - col[:, None], -(WIN_W - 1), WIN_W - 1) + (WIN_W - 1)
    n_loc = win_h * GRID_W

    def row_block(r):
        r0 = jnp.clip(r - win_h // 2, 0, rows - win_h)
        qr = lax.dynamic_index_in_dim(qg, r, axis=1, keepdims=False)
        kr = lax.dynamic_slice_in_dim(kg, r0, win_h, axis=1)
        vr = lax.dynamic_slice_in_dim(vg, r0, win_h, axis=1)
        s_loc = jnp.einsum('bqhd,bkjhd->bhqkj', qr, kr, preferred_element_type=jnp.float32) * scale
        row_off = r0 + jnp.arange(win_h) - r + (WIN_H - 1)
        bias = rpb[:, row_off[:, None, None], col_off[None, :, :]]
        bias = bias.transpose(0, 2, 1, 3).astype(jnp.float32)
        s_loc = jnp.where(col_valid[None, None, :, None, :], s_loc + bias[None], NEG_INF)
        s_ctx = jnp.einsum('bqhd,bmhd->bhqm', qr, kc, preferred_element_type=jnp.float32) * scale
        s = jnp.concatenate([s_loc.reshape(B, H, GRID_W, n_loc), s_ctx], axis=-1)
        p = jax.nn.softmax(s, axis=-1).astype(v.dtype)
        p_loc = p[..., :n_loc].reshape(B, H, GRID_W, win_h, GRID_W)
        p_ctx = p[..., n_loc:]
        return (jnp.einsum('bhqkj,bkjhd->bqhd', p_loc, vr)
                + jnp.einsum('bhqm,bmhd->bqhd', p_ctx, vc))

    out = lax.map(row_block, jnp.arange(rows))
    return out.transpose(1, 0, 2, 3, 4).reshape(B, N, H * Dh)


def context_attention(qc, kc, vc):
    s = jnp.einsum('blhd,bmhd->bhlm', qc, kc, preferred_element_type=jnp.float32) * (qc.shape[-1] ** -0.5)
    p = jax.nn.softmax(s, axis=-1).astype(vc.dtype)
    o = jnp.einsum('bhlm,bmhd->blhd', p, vc)
    return o.reshape(qc.shape[0], qc.shape[1], -1)


def na_mixer(h, hc, w_qkv, w_o, rpb, with_ctx):
    B, N, D = h.shape
    Lc = hc.shape[1]
    qkv = (h @ w_qkv).reshape(B, N, 3, NA_HEADS, NA_HEAD_DIM)
    kvc = (hc @ w_qkv[:, D:]).reshape(B, Lc, 2, NA_HEADS, NA_HEAD_DIM)
    kc, vc = kvc[:, :, 0], kvc[:, :, 1]
    y = neighbourhood_attention(qkv[:, :, 0], qkv[:, :, 1], qkv[:, :, 2], kc, vc, rpb) @ w_o
    yc = None
    if with_ctx:
        qc = (hc @ w_qkv[:, :D]).reshape(B, Lc, NA_HEADS, NA_HEAD_DIM)
        yc = context_attention(qc, kc, vc) @ w_o
    return y, yc


def conv_module(h, w_pw1, b_pw1, w_dw, b_dw, ln_g, ln_b, w_pw2, b_pw2):
    a, g = jnp.split(h @ w_pw1 + b_pw1, 2, axis=-1)
    u = a * jax.nn.sigmoid(g)
    u = _depthwise_conv(u, w_dw, b_dw)
    u = jax.nn.silu(_layer_norm(u, ln_g, ln_b))
    return u @ w_pw2 + b_pw2


def conv_ffn(h, w_up, w_dw, b_dw, w_down):
    u = _depthwise_conv(h @ w_up, w_dw, b_dw)
    g, v = jnp.split(u, 2, axis=-1)
    return (jax.nn.silu(g) * v) @ w_down


def setup_inputs(seed: int = 0) -> dict:
    key = jax.random.key(seed)
    ks = jax.random.split(key, 24)

    def nrm(k, shape, s):
        return jax.random.normal(k, shape, jnp.float32) * s

    D, F2 = D_MODEL, 2 * D_FF
    return {
        'x': nrm(ks[0], (BATCH, SEQ, D), 1.0),
        'c': nrm(ks[1], (BATCH, D), 1.0),
        'ctx': nrm(ks[2], (BATCH, CTX_LEN, D), 1.0),
        'c_ctx': nrm(ks[3], (D,), 1.0),
        'w_mod': nrm(ks[4], (DEPTH, D, N_MOD * D), 0.5 * D ** -0.5),
        'b_mod': nrm(ks[5], (DEPTH, N_MOD * D), 0.02),
        'pool_w': nrm(ks[6], (N_POOL_LAYERS, N_POOL_GROUPS, POOL_GROUP, POOL_GROUP), POOL_GROUP ** -0.5),
        'pool_scale': 1.0 + nrm(ks[7], (N_POOL_LAYERS, D), 0.1),
        'na_w_qkv': nrm(ks[8], (N_NA_LAYERS, D, 3 * D), D ** -0.5),
        'na_w_o': nrm(ks[9], (N_NA_LAYERS, D, D), D ** -0.5),
        'na_rpb': nrm(ks[10], (N_NA_LAYERS, NA_HEADS, 2 * WIN_H - 1, 2 * WIN_W - 1), 0.5),
        'cv_w_pw1': nrm(ks[11], (N_CONV_LAYERS, D, 2 * D), D ** -0.5),
        'cv_b_pw1': nrm(ks[12], (N_CONV_LAYERS, 2 * D), 0.02),
        'cv_w_dw': nrm(ks[13], (N_CONV_LAYERS, CONV_WIDTH, D), CONV_WIDTH ** -0.5),
        'cv_b_dw': nrm(ks[14], (N_CONV_LAYERS, D), 0.02),
        'cv_ln_g': 1.0 + nrm(ks[15], (N_CONV_LAYERS, D), 0.1),
        'cv_ln_b': nrm(ks[16], (N_CONV_LAYERS, D), 0.02),
        'cv_w_pw2': nrm(ks[17], (N_CONV_LAYERS, D, D), D ** -0.5),
        'cv_b_pw2': nrm(ks[18], (N_CONV_LAYERS, D), 0.02),
        'ffn_w_up': nrm(ks[19], (DEPTH, D, F2), D ** -0.5),
        'ffn_w_dw': nrm(ks[20], (DEPTH, FFN_CONV_WIDTH, F2), FFN_CONV_WIDTH ** -0.5),
        'ffn_b_dw': nrm(ks[21], (DEPTH, F2), 0.02),
        'ffn_w_down': nrm(ks[22], (DEPTH, D_FF, D), D_FF ** -0.5),
        'final_norm_g': 1.0 + nrm(ks[23], (D,), 0.1),
    }


def reference(x, c, ctx, c_ctx, w_mod, b_mod, pool_w, pool_scale, na_w_qkv, na_w_o, na_rpb,
              cv_w_pw1, cv_b_pw1, cv_w_dw, cv_b_dw, cv_ln_g, cv_ln_b, cv_w_pw2, cv_b_pw2,
              ffn_w_up, ffn_w_dw, ffn_b_dw, ffn_w_down, final_norm_g):
    for i in range(DEPTH):
        kind, j = i % N_MIXERS, i // N_MIXERS
        update_ctx = i < DEPTH - 1
        need_ctx = update_ctx or kind == 1
        m = _modulation(c, w_mod[i], b_mod[i])
        h = _ada_in(x, m, 0)
        if need_ctx:
            mc = _modulation(c_ctx[None], w_mod[i], b_mod[i])
            hc = _ada_in(ctx, mc, 0)
        if kind == 0:
            y = pool_mixer(h, pool_w[j], pool_scale[j])
            yc = pool_mixer(hc, pool_w[j], pool_scale[j]) if update_ctx else None
        elif kind == 1:
            y, yc = na_mixer(h, hc, na_w_qkv[j], na_w_o[j], na_rpb[j], update_ctx)
        else:
            cv = (cv_w_pw1[j], cv_b_pw1[j], cv_w_dw[j], cv_b_dw[j], cv_ln_g[j], cv_ln_b[j],
                  cv_w_pw2[j], cv_b_pw2[j])
            y = conv_module(h, *cv)
            yc = conv_module(hc, *cv) if update_ctx else None
        x = x + m[:, 2] * y
        ff = (ffn_w_up[i], ffn_w_dw[i], ffn_b_dw[i], ffn_w_down[i])
        x = x + m[:, 5] * conv_ffn(_ada_in(x, m, 3), *ff)
        if update_ctx:
            ctx = ctx + mc[:, 2] * yc
            ctx = ctx + mc[:, 5] * conv_ffn(_ada_in(ctx, mc, 3), *ff)
    return _rms_norm(x) * final_norm_g
```

```python
import numpy as np
from contextlib import ExitStack
import concourse.bass as bass
import concourse.mybir as mybir
from concourse.bass_utils import run_bass_kernel_spmd

F32 = mybir.dt.float32
BF16 = mybir.dt.bfloat16
AF = mybir.ActivationFunctionType
ALU = mybir.AluOpType

D = 2048
KC = 16
DFF = 5632
FC = 44
SEQ = 16384
NB = 2
NCORE = 8
OWN = 4096
HALO = 384
TE = OWN + 2 * HALO
O0 = HALO
E0 = HALO + OWN
CTXL = 256
CPAD = 16
TEC = CTXL + 2 * CPAD
GRID_W = 64
ROWS = SEQ // GRID_W
NHEAD = 32
NPAIR = 16
EPS = 1e-6
LN_EPS = 1e-5
NEG = -30000.0
NQB = 34
QB0 = 2
NTAB = 27
TAB_OFF = {"I": 0, "A": 5, "B": 11, "C": 16, "D": 21}


class _Op:
    __slots__ = ("stream", "fn", "dma", "deps", "signal", "sig", "slot", "semval", "prev")

    def __init__(self, stream, fn, dma):
        self.stream = stream
        self.fn = fn
        self.dma = dma
        self.deps = set()
        self.signal = False
        self.sig = 0
        self.slot = None
        self.semval = 0
        self.prev = None


class Sched:
    STREAMS = ("pe", "act", "dve", "pool", "sp")
    NSLOT = {"sp": 12, "pool": 8}

    def __init__(self, nc):
        self.nc = nc
        self.eng = {"pe": nc.tensor, "act": nc.scalar, "dve": nc.vector, "pool": nc.gpsimd, "sp": nc.sync}
        self.batch = []
        self.last_w = {}
        self.readers = {}
        self.cnt = {s: 0 for s in self.STREAMS}
        self.dcnt = {s: 0 for s in self.STREAMS}
        self.slots = {}
        self.sem_c = {s: nc.alloc_semaphore("c_" + s) for s in self.STREAMS}
        self.sem_d = {}
        self.waited = {s: {} for s in self.STREAMS}
        self.bar_deps = []
        self.bar_pending = {s: False for s in self.STREAMS}
        self.last_op = {s: None for s in self.STREAMS}
        self.n_ops = 0
        self.n_waits = 0

    def op(self, stream, fn, r=(), w=(), dma=False):
        o = _Op(stream, fn, dma)
        deps = o.deps
        lw = self.last_w
        rd = self.readers
        for k in r:
            d = lw.get(k)
            if d is not None:
                deps.add(d)
        for k in w:
            d = lw.get(k)
            if d is not None:
                deps.add(d)
            x = rd.get(k)
            if x:
                deps.update(x.values())
        rkey = (stream, len(self.batch)) if dma else stream
        for k in r:
            x = rd.get(k)
            if x is None:
                x = rd[k] = {}
            x[rkey] = o
        for k in w:
            lw[k] = o
            rd[k] = {}
        deps.discard(o)
        if self.bar_pending[stream]:
            deps.update(self.bar_deps)
            self.bar_pending[stream] = False
        self.batch.append(o)
        return o

    def dma(self, stream, out, in_, r=(), w=()):
        return self.op(stream, lambda e: e.dma_start(out=out, in_=in_), r, w, dma=True)

    def flush(self):
        for o in self.batch:
            for d in o.deps:
                if (not d.dma) and d.stream != o.stream:
                    d.signal = True
        last = {}
        for o in self.batch:
            if not o.dma:
                last[o.stream] = o
        for o in last.values():
            o.signal = True
        for o in self.batch:
            if o.dma:
                n = self.NSLOT[o.stream]
                k = self.dcnt[o.stream]
                self.dcnt[o.stream] += 1
                key = (o.stream, k % n)
                st = self.slots.get(key)
                if st is None:
                    st = self.slots[key] = [None, 0]
                    self.sem_d[key] = self.nc.alloc_semaphore("d_%s_%d" % key)
                o.prev = st[0]
                st[1] += 16
                o.slot = key
                o.semval = st[1]
                st[0] = o
            else:
                if o.signal:
                    self.cnt[o.stream] += 1
                    o.sig = self.cnt[o.stream]
                self.last_op[o.stream] = o
        for o in self.batch:
            e = self.eng[o.stream]
            wt = self.waited[o.stream]
            need = {}
            deps = list(o.deps)
            if o.dma and o.prev is not None:
                deps.append(o.prev)
            for d in deps:
                if d.dma:
                    key = ("d", d.slot)
                    v = d.semval
                elif d.stream != o.stream:
                    key = ("c", d.stream)
                    v = d.sig
                else:
                    continue
                if need.get(key, 0) < v:
                    need[key] = v
            for key, v in need.items():
                if wt.get(key, 0) >= v:
                    continue
                wt[key] = v
                sem = self.sem_d[key[1]] if key[0] == "d" else self.sem_c[key[1]]
                e.wait_ge(sem, v)
                self.n_waits += 1
            ins = o.fn(e)
            if o.dma:
                ins.then_inc(self.sem_d[o.slot], 16)
            elif o.signal:
                ins.then_inc(self.sem_c[o.stream], 1)
            o.fn = None
            o.deps = None
        self.n_ops += len(self.batch)
        self.batch = []
        self.last_w = {}
        self.readers = {}
        self.bar_deps = [o for o in self.last_op.values() if o is not None] + [st[0] for st in self.slots.values()]
        self.bar_pending = {s: True for s in self.STREAMS}

    def finish(self):
        self.flush()
        for key, st in self.slots.items():
            self.nc.sync.wait_ge(self.sem_d[key], st[1])


def _tiles(a, b, tmax):
    n = -(-(b - a) // tmax)
    base, rem = divmod(b - a, n)
    out = []
    t = a
    for i in range(n):
        sz = base + (1 if i < rem else 0)
        out.append((t, t + sz))
        t += sz
    return out


def _nblocks(W):
    return [(n0, min(n0 + 512, W)) for n0 in range(0, W, 512)]


class Builder:
    def __init__(self, n_layers=4, dbg=None):
        self.n_layers = n_layers
        self.dbg = dbg
        nc = self.nc = bass.Bass("TRN2", target_bir_lowering=False)
        self.S = Sched(nc)
        dt = nc.dram_tensor

        def inp(name, shape, dtype=F32):
            return dt(name, list(shape), dtype, kind="ExternalInput").ap()

        def scr(name, shape, dtype=F32):
            return dt(name, list(shape), dtype, kind="Internal").ap()

        self.xin = inp("xin", [D, TE])
        self.cin = inp("cin", [D, TEC])
        self.vmask = inp("vmask", [128, TE])
        self.rcnt = inp("rcnt", [128, 4, TE])
        self.vmask_c = inp("vmask_c", [128, TEC])
        self.rcnt_c = inp("rcnt_c", [128, 4, TEC])
        self.cs = inp("cs", [128, KC, 2])
        self.bmod = inp("bmod", [4, 128, 96])
        self.w_mod = inp("w_mod", [4, D, 6 * D])
        self.pool_w = inp("pool_w", [2, 4, 512, 512])
        self.pool_ls = inp("pool_ls", [2, 128, KC])
        self.w_qkv = inp("w_qkv", [D, 3 * D])
        self.w_o = inp("w_o", [D, D])
        self.btab = inp("btab", [NHEAD, NTAB, 128, 128])
        self.w_pw1 = inp("w_pw1", [D, 2 * D])
        self.w_pw2 = inp("w_pw2", [D, D])
        self.cvv = inp("cvv", [128, KC, 40])
        self.w_up = inp("w_up", [4, D, 2 * DFF])
        self.w_dn = inp("w_dn", [4, DFF, D])
        self.cw = inp("cw", [128, 4, 2 * FC, 4])
        self.fng = inp("fng", [128, KC])
        self.ident_in = inp("ident_in", [128, 128])
        self.outT = dt("outT", [D, OWN], F32, kind="ExternalOutput").ap()
        self.xa = scr("xa", [D, TE])
        self.xb = scr("xb", [D, TE])
        self.ca = scr("ca", [D, TEC])
        self.cb = scr("cb", [D, TEC])
        self.QT = scr("QT", [D, TE], BF16)
        self.KT = scr("KT", [D, TE], BF16)
        self.VA = scr("VA", [NPAIR, TE, 130], BF16)
        self.KcT = scr("KcT", [D, CTXL], BF16)
        self.VcA = scr("VcA", [NPAIR, CTXL, 130], BF16)
        self.OT = scr("OT", [D, TE], BF16)
        self.dbg_outs = {}
        for name in (dbg.split(",") if dbg else []):
            t = getattr(self, name)
            self.dbg_outs[name] = dt("dbg_" + name, list(t.shape), t.dtype, kind="ExternalOutput").ap()

    def sb(self, st, name, shape, dtype):
        self.uid = getattr(self, "uid", 0) + 1
        return st.enter_context(self.nc.sbuf_tensor("%s_%d" % (name, self.uid), list(shape), dtype))

    def ps(self, st, name, shape, dtype=F32):
        self.uid = getattr(self, "uid", 0) + 1
        return st.enter_context(self.nc.psum_tensor("%s_%d" % (name, self.uid), list(shape), dtype))

    def consts(self, st):
        S = self.S
        self.ones = self.sb(st, "ones", [128, 128], BF16)
        self.ident = self.sb(st, "ident", [128, 128], BF16)
        self.M = [self.sb(st, "M%d" % l, [128, 2, 96], F32) for l in range(4)]
        self.cws = self.sb(st, "cws", [128, 4, 2 * FC, 4], F32)
        self.cvs = self.sb(st, "cvs", [128, KC, 40], F32)
        self.fngs = self.sb(st, "fngs", [128, KC], F32)
        self.pls = self.sb(st, "pls", [128, 2, KC], F32)
        self.gls = self.sb(st, "gls", [128, KC], F32)
        S.op("dve", lambda e: e.memset(self.ones[:, :], 1.0), w=["ones"])
        S.dma("sp", self.cws[:, :, :, :], self.cw, w=["cws"])
        S.dma("sp", self.cvs[:, :, :], self.cvv, w=["cvs"])
        S.dma("sp", self.fngs[:, :], self.fng, w=["fngs"])
        S.dma("pool", self.ident[:, :], self.ident_in, w=["ident"])
        for j in range(2):
            S.dma("sp", self.pls[:, j, :], self.pool_ls[j], w=["pls"])

    def mod(self, l, who, k, c):
        return self.M[l][:, who, k * KC + c:k * KC + c + 1]

    def phase_mod(self):
        S, nc = self.S, self.nc
        with ExitStack() as st:
            c32 = self.sb(st, "c32", [128, KC, 2], F32)
            sg = self.sb(st, "csg", [128, KC, 2], F32)
            cb = self.sb(st, "cbf", [128, KC, 2], BF16)
            bm = self.sb(st, "bm", [128, 4, 96], F32)
            wb = [self.sb(st, "wm%d" % i, [128, KC, 512], BF16) for i in range(3)]
            pm = [self.ps(st, "pm%d" % i, [128, 512]) for i in range(2)]
            S.dma("sp", c32[:, :, :], self.cs, w=["c32"])
            for l in range(4):
                S.dma("sp", bm[:, l, :], self.bmod[l], w=["bm"])
            S.op("act", lambda e: e.activation(out=sg[:, :, :], in_=c32[:, :, :], func=AF.Sigmoid), r=["c32"], w=["sg"])
            S.op("dve", lambda e: e.tensor_tensor(out=cb[:, :, :], in0=c32[:, :, :], in1=sg[:, :, :], op=ALU.mult),
                 r=["c32", "sg"], w=["cb"])
            nld = 0
            for l in range(self.n_layers):
                p = pm[l % 2]
                for cg in range(24):
                    s = nld % 3
                    nld += 1
                    S.dma("pool", wb[s][:, :, :],
                          self.w_mod[l][:, cg * 512:(cg + 1) * 512].rearrange("(k p) c -> p k c", p=128),
                          w=[("wm", s)])
                    for f4 in range(4):
                        fo = cg * 4 + f4
                        for k in range(KC):
                            S.op("pe", (lambda e, p=p, s=s, f4=f4, fo=fo, k=k: e.matmul(
                                p[:, fo * 2:fo * 2 + 2], lhsT=wb[s][:, k, f4 * 128:(f4 + 1) * 128], rhs=cb[:, k, :],
                                start=(k == 0), stop=(k == KC - 1))),
                                r=[("wm", s), "cb"], w=[("pm", l % 2)])
                for who in range(2):
                    S.op("dve", (lambda e, p=p, l=l, who=who: e.tensor_tensor(
                        out=self.M[l][:, who, :], in0=p[:, 0:192].rearrange("p (f w) -> p f w", w=2)[:, :, who],
                        in1=bm[:, l, :], op=ALU.add)), r=[("pm", l % 2), "bm"], w=[("M", l)])
                    for k in (1, 4):
                        S.op("dve", (lambda e, l=l, who=who, k=k: e.tensor_scalar_add(
                            out=self.M[l][:, who, k * KC:(k + 1) * KC], in0=self.M[l][:, who, k * KC:(k + 1) * KC],
                            scalar1=1.0)), r=[("M", l)], w=[("M", l)])
            S.flush()

    def norm_stats(self, src, c0, c1, bufs, ps, pkey, vm_dram=None, rstd=None, rkey="rstd"):
        S = self.S
        W = c1 - c0
        xg, sq = bufs["xg"], bufs["sq"]
        if rstd is None:
            rstd = bufs["rstd"]
        gs = bufs["gs"]
        ng = KC // gs
        for g in range(ng):
            s = g % 2
            S.dma("sp", xg[s][:, :, :W],
                  src[g * gs * 128:(g + 1) * gs * 128, c0:c1].rearrange("(k p) t -> p k t", p=128), w=[("xg", s)])
            S.op("act", (lambda e, s=s: e.activation(out=sq[s][:, :, :W], in_=xg[s][:, :, :W], func=AF.Square)),
                 r=[("xg", s)], w=[("sq", s)])
            for kk in range(gs):
                for (n0, n1) in _nblocks(W):
                    S.op("pe", (lambda e, s=s, kk=kk, n0=n0, n1=n1, g=g: e.matmul(
                        ps[:, n0:n1], lhsT=self.ones[:, :], rhs=sq[s][:, kk, n0:n1],
                        start=(g == 0 and kk == 0), stop=(g == ng - 1 and kk == gs - 1))),
                        r=[("sq", s), "ones"], w=[pkey])
        S.op("dve", (lambda e: e.tensor_scalar(out=rstd[:, :W], in0=ps[:, :W], scalar1=1.0 / D, scalar2=EPS,
                                               op0=ALU.mult, op1=ALU.add)), r=[pkey], w=[rkey])
        S.op("act", (lambda e: e.activation(out=rstd[:, :W], in_=rstd[:, :W], func=AF.Sqrt)), r=[rkey], w=[rkey])
        S.op("dve", (lambda e: e.reciprocal(out=rstd[:, :W], in_=rstd[:, :W])), r=[rkey], w=[rkey])
        if vm_dram is not None:
            vm = bufs["vm"]
            S.dma("sp", vm[:, :W], vm_dram[:, c0:c1], w=["vm"])
            S.op("dve", (lambda e: e.tensor_tensor(out=rstd[:, :W], in0=rstd[:, :W], in1=vm[:, :W], op=ALU.mult)),
                 r=[rkey, "vm"], w=[rkey])

    def ada_h(self, src, c0, c1, bufs, l, who, kshift, hout, masked):
        S = self.S
        W = c1 - c0
        xg, rstd = bufs["xg"], bufs["rstd"]
        gs = bufs["gs"]
        for g in range(KC // gs):
            s = g % 2
            S.dma("sp", xg[s][:, :, :W],
                  src[g * gs * 128:(g + 1) * gs * 128, c0:c1].rearrange("(k p) t -> p k t", p=128), w=[("xg", s)])
            for kk in range(gs):
                k = gs * g + kk
                S.op("dve", (lambda e, s=s, kk=kk: e.tensor_tensor(
                    out=xg[s][:, kk, :W], in0=xg[s][:, kk, :W], in1=rstd[:, :W], op=ALU.mult)),
                    r=[("xg", s), "rstd"], w=[("xg", s)])
                S.op("act", (lambda e, s=s, kk=kk, k=k: e.activation(
                    out=hout[:, k, :W], in_=xg[s][:, kk, :W], func=AF.Identity,
                    bias=self.mod(l, who, kshift, k), scale=self.mod(l, who, kshift + 1, k))),
                    r=[("xg", s), ("M", l)], w=[("h", k)])
                if masked:
                    S.op("dve", (lambda e, k=k: e.tensor_tensor(
                        out=hout[:, k, :W], in0=hout[:, k, :W], in1=bufs["vm"][:, :W], op=ALU.mult)),
                        r=[("h", k), "vm"], w=[("h", k)])

    def norm_bufs(self, st, wmax, with_vm=True, gs=2):
        b = {"xg": [self.sb(st, "xg%d" % i, [128, gs, wmax], F32) for i in range(2)],
             "sq": [self.sb(st, "sq%d" % i, [128, gs, wmax], BF16) for i in range(2)],
             "rstd": self.sb(st, "rstd", [128, wmax], F32), "gs": gs}
        if with_vm:
            b["vm"] = self.sb(st, "vm", [128, wmax], F32)
        return b

    def mm_fm(self, ps, pkey, wslot, wkey, nk, rhs, rkeyf, W, wcol0=0):
        S = self.S
        for (n0, n1) in _nblocks(W):
            for k in range(nk):
                S.op("pe", (lambda e, k=k, n0=n0, n1=n1: e.matmul(
                    ps[:, n0:n1], lhsT=wslot[:, k, wcol0:wcol0 + 128], rhs=rhs[:, k, n0:n1],
                    start=(k == 0), stop=(k == nk - 1))),
                    r=[wkey, rkeyf(k)], w=[pkey])

    def phase_pool(self, l, j, who, src, dst, a, b, tmax, inval, vm_dram, rc_dram):
        S, nc = self.S, self.nc
        with ExitStack() as st:
            wmax = tmax + 16
            nb = self.norm_bufs(st, wmax)
            pw = self.sb(st, "pw", [128, 4, 4, 512], BF16)
            dd = self.sb(st, "dd", [128, KC, tmax], BF16)
            hn = [self.sb(st, "hn%d" % i, [128, wmax], F32) for i in range(2)]
            sa = [self.sb(st, "sa%d" % i, [128, wmax], F32) for i in range(2)]
            sb2 = [self.sb(st, "sbb%d" % i, [128, wmax], F32) for i in range(2)]
            rc = self.sb(st, "rc", [128, 4, tmax], F32)
            xe = [self.sb(st, "xe%d" % i, [128, tmax], F32) for i in range(2)]
            xo = [self.sb(st, "xo%d" % i, [128, tmax], F32) for i in range(2)]
            ps = [self.ps(st, "pp%d" % i, [128, 1024]) for i in range(4)]
            for g in range(4):
                S.dma("pool", pw[:, g, :, :], self.pool_w[j][g].rearrange("(c p) e -> p c e", p=128), w=["pw"])
            S.op("dve", lambda e: e.tensor_tensor(out=self.gls[:, :], in0=self.M[l][:, who, 2 * KC:3 * KC],
                                                  in1=self.pls[:, j, :], op=ALU.mult),
                 r=[("M", l), "pls"], w=["gls"])
            tl = _tiles(a, b, tmax)
            rst2 = [nb["rstd"], self.sb(st, "rstdB", [128, wmax], F32)]

            def stats(i):
                t0, t1 = tl[i]
                c0, c1 = t0 - 8, t1 + 8
                masked = c0 < inval[0] or c1 > inval[1]
                self.norm_stats(src, c0, c1, nb, ps[0], ("ps", 0), vm_dram if masked else None,
                                rstd=rst2[i % 2], rkey=("rstd", i % 2))

            def body(i):
                t0, t1 = tl[i]
                Tt = t1 - t0
                c0, c1 = t0 - 8, t1 + 8
                W = c1 - c0
                rstd_i, rkey_i = rst2[i % 2], ("rstd", i % 2)
                S.dma("sp", rc[:, :, :Tt], rc_dram[:, :, t0:t1], w=["rc"])
                xg = nb["xg"]
                for g in range(8):
                    s = g % 2
                    S.dma("sp", xg[s][:, :, :W], src[g * 256:(g + 1) * 256, c0:c1].rearrange("(k p) t -> p k t", p=128),
                          w=[("xg", s)])
                    for kk in range(2):
                        k = 2 * g + kk
                        grp = k // 4
                        en = "dve"
                        h_, a_, b_ = hn[kk], sa[kk], sb2[kk]
                        hk, ak, bk = ("hn", kk), ("sa", kk), ("sb", kk)
                        S.op(en, (lambda e, s=s, kk=kk, h_=h_: e.tensor_tensor(
                            out=h_[:, :W], in0=xg[s][:, kk, :W], in1=rstd_i[:, :W], op=ALU.mult)),
                            r=[("xg", s), rkey_i], w=[hk])
                        S.op(en, (lambda e, h_=h_, a_=a_: e.tensor_tensor(
                            out=a_[:, 1:W], in0=h_[:, 0:W - 1], in1=h_[:, 1:W], op=ALU.add)), r=[hk], w=[ak])
                        cur, ck, oth, ok = a_, ak, b_, bk
                        lo, hi = 1, W
                        sh = 1
                        for step in range(grp):
                            nlo, nhi = lo + sh, hi - sh
                            S.op(en, (lambda e, cur=cur, oth=oth, nlo=nlo, nhi=nhi, sh=sh: e.tensor_tensor(
                                out=oth[:, nlo:nhi], in0=cur[:, nlo - sh:nhi - sh], in1=cur[:, nlo + sh:nhi + sh],
                                op=ALU.add)), r=[ck], w=[ok])
                            cur, ck, oth, ok = oth, ok, cur, ck
                            lo, hi = nlo, nhi
                            sh *= 2
                        S.op(en, (lambda e, cur=cur, oth=oth, grp=grp: e.tensor_tensor(
                            out=oth[:, 8:8 + Tt], in0=cur[:, 8:8 + Tt], in1=rc[:, grp, :Tt], op=ALU.mult)),
                            r=[ck, "rc"], w=[ok])
                        S.op(en, (lambda e, oth=oth, h_=h_: e.tensor_tensor(
                            out=oth[:, 8:8 + Tt], in0=oth[:, 8:8 + Tt], in1=h_[:, 8:8 + Tt], op=ALU.subtract)),
                            r=[ok, hk], w=[ok])
                        S.op("act", (lambda e, oth=oth, k=k: e.activation(
                            out=dd[:, k, :Tt], in_=oth[:, 8:8 + Tt], func=AF.Copy, scale=self.mod(l, who, 1, k))),
                            r=[ok, ("M", l)], w=[("dd", k)])
                if i + 1 < len(tl):
                    stats(i + 1)
                for ec in range(KC):
                    grp = ec // 4
                    s = ec % 2
                    p = ps[1 + ec % 3]
                    pk = ("ps", 1 + ec % 3)
                    for (n0, n1) in _nblocks(Tt):
                        for cc in range(4):
                            S.op("pe", (lambda e, p=p, grp=grp, cc=cc, ec=ec, n0=n0, n1=n1: e.matmul(
                                p[:, n0:n1], lhsT=pw[:, grp, cc, (ec % 4) * 128:(ec % 4 + 1) * 128],
                                rhs=dd[:, grp * 4 + cc, n0:n1], start=(cc == 0), stop=(cc == 3))),
                                r=["pw", ("dd", grp * 4 + cc)], w=[pk])
                    S.dma("sp", xe[s][:, :Tt], src[ec * 128:(ec + 1) * 128, t0:t1], w=[("xe", s)])
                    S.op("dve", (lambda e, p=p, s=s, ec=ec: e.scalar_tensor_tensor(
                        out=xo[s][:, :Tt], in0=p[:, :Tt], scalar=self.gls[:, ec:ec + 1], in1=xe[s][:, :Tt],
                        op0=ALU.mult, op1=ALU.add)), r=[pk, "gls", ("xe", s)], w=[("xo", s)])
                    S.dma("sp", dst[ec * 128:(ec + 1) * 128, t0:t1], xo[s][:, :Tt], r=[("xo", s)], w=[("dst", ec)])
            stats(0)
            for i in range(len(tl)):
                body(i)
            S.flush()

    def phase_ffn(self, l, who, src, dst, a, b, tmax, inval, vm_dram):
        S, nc = self.S, self.nc
        with ExitStack() as st:
            wmax = tmax + 2
            nb = self.norm_bufs(st, wmax, gs=1)
            h2 = self.sb(st, "h2", [128, KC, wmax], BF16)
            aa = self.sb(st, "aa", [128, FC, tmax], BF16)
            wup = [self.sb(st, "wup%d" % i, [128, KC, 256], BF16) for i in range(2)]
            wdn = [self.sb(st, "wdn%d" % i, [128, FC, 128], BF16) for i in range(2)]
            tg = [self.sb(st, "tg%d" % i, [128, tmax], F32) for i in range(2)]
            tv = [self.sb(st, "tv%d" % i, [128, tmax], F32) for i in range(2)]
            ps = [self.ps(st, "pf%d" % i, [128, 1024]) for i in range(4)]
            xe, xo = tg, tv
            tl = _tiles(a, b, tmax)

            def prep(i, stage):
                t0, t1 = tl[i]
                c0, c1 = t0 - 1, t1 + 1
                masked = c0 < inval[0] or c1 > inval[1]
                if stage == 0:
                    self.norm_stats(src, c0, c1, nb, ps[3], ("ps", 3), vm_dram if masked else None)
                else:
                    self.ada_h(src, c0, c1, nb, l, who, 3, h2, masked)

            def body(i):
                t0, t1 = tl[i]
                Tt = t1 - t0
                W = Tt + 2
                for j in range(FC):
                    s = j % 2
                    for half, col in enumerate((j * 128, DFF + j * 128)):
                        S.dma("pool", wup[s][:, :, half * 128:(half + 1) * 128],
                              self.w_up[l][:, col:col + 128].rearrange("(k p) c -> p k c", p=128),
                              w=[("wup", s, half)])
                    for half in range(2):
                        pi = (2 * j + half) % 4
                        self.mm_fm(ps[pi], ("ps", pi), wup[s], ("wup", s, half), KC, h2, lambda k: ("h", k), W,
                                   wcol0=half * 128)
                    for half, tt, tk in ((0, tg[s], ("tg", s)), (1, tv[s], ("tv", s))):
                        pi = (2 * j + half) % 4
                        pt = ps[pi]
                        ci = half * FC + j
                        S.op("act", (lambda e, pt=pt, tt=tt, ci=ci: e.activation(
                            out=tt[:, :Tt], in_=pt[:, 1:Tt + 1], func=AF.Identity,
                            bias=self.cws[:, l, ci, 3:4], scale=self.cws[:, l, ci, 1:2])),
                            r=[("ps", pi), "cws"], w=[tk])
                        S.op("dve", (lambda e, pt=pt, tt=tt, ci=ci: e.scalar_tensor_tensor(
                            out=tt[:, :Tt], in0=pt[:, 0:Tt], scalar=self.cws[:, l, ci, 0:1], in1=tt[:, :Tt],
                            op0=ALU.mult, op1=ALU.add)), r=[("ps", pi), "cws", tk], w=[tk])
                        S.op("dve", (lambda e, pt=pt, tt=tt, ci=ci: e.scalar_tensor_tensor(
                            out=tt[:, :Tt], in0=pt[:, 2:Tt + 2], scalar=self.cws[:, l, ci, 2:3], in1=tt[:, :Tt],
                            op0=ALU.mult, op1=ALU.add)), r=[("ps", pi), "cws", tk], w=[tk])
                    S.op("act", (lambda e, s=s: e.activation(out=tg[s][:, :Tt], in_=tg[s][:, :Tt], func=AF.Silu)),
                         r=[("tg", s)], w=[("tg", s)])
                    S.op("dve", (lambda e, s=s, j=j: e.tensor_tensor(
                        out=aa[:, j, :Tt], in0=tg[s][:, :Tt], in1=tv[s][:, :Tt], op=ALU.mult)),
                        r=[("tg", s), ("tv", s)], w=[("aa", j)])
                for ec in range(KC):
                    s = ec % 2
                    pi = ec % 4
                    S.dma("pool", wdn[s][:, :, :],
                          self.w_dn[l][:, ec * 128:(ec + 1) * 128].rearrange("(j p) c -> p j c", p=128),
                          w=[("wdn", s)])
                    self.mm_fm(ps[pi], ("ps", pi), wdn[s], ("wdn", s), FC, aa, lambda k: ("aa", k), Tt)
                    S.dma("sp", xe[s][:, :Tt], src[ec * 128:(ec + 1) * 128, t0:t1], w=[("tg", s)])
                    S.op("dve", (lambda e, s=s, pi=pi, ec=ec: e.scalar_tensor_tensor(
                        out=xo[s][:, :Tt], in0=ps[pi][:, :Tt], scalar=self.mod(l, who, 5, ec), in1=xe[s][:, :Tt],
                        op0=ALU.mult, op1=ALU.add)), r=[("ps", pi), ("M", l), ("tg", s)], w=[("tv", s)])
                    S.dma("sp", dst[ec * 128:(ec + 1) * 128, t0:t1], xo[s][:, :Tt], r=[("tv", s)], w=[("dst", ec)])
                    if i + 1 < len(tl):
                        if ec == 2:
                            prep(i + 1, 0)
                        if ec == 6:
                            prep(i + 1, 1)
            prep(0, 0)
            prep(0, 1)
            for i in range(len(tl)):
                body(i)
            S.flush()

    def phase_qkv(self, who, src, tiles, ctx):
        S = self.S
        l = 1
        with ExitStack() as st:
            tmax = 1024
            nb = self.norm_bufs(st, tmax, with_vm=False, gs=2)
            h = self.sb(st, "hq", [128, KC, tmax], BF16)
            wq = [self.sb(st, "wq%d" % i, [128, KC, 256], BF16) for i in range(2)]
            wv = [self.sb(st, "wv%d" % i, [128, KC, 512], BF16) for i in range(2)]
            qk = [self.sb(st, "qk%d" % i, [128, tmax], BF16) for i in range(2)]
            vst = [self.sb(st, "vst%d" % i, [128, 8, 65], BF16) for i in range(2)]
            ps = [self.ps(st, "pq%d" % i, [128, 1024]) for i in range(4)]
            for i in range(2):
                S.op("dve", (lambda e, i=i: e.memset(vst[i][:, :, :], 1.0)), w=[("vst", i)])

            def body(t0, t1):
                Tt = t1 - t0
                self.norm_stats(src, t0, t1, nb, ps[0], ("ps", 0))
                self.ada_h(src, t0, t1, nb, l, who, 0, h, False)
                fo0 = 16 if ctx else 0
                for pidx, fo2 in enumerate(range(fo0, 32, 2)):
                    s = pidx % 2
                    S.dma("pool", wq[s][:, :, :],
                          self.w_qkv[:, fo2 * 128:fo2 * 128 + 256].rearrange("(k p) c -> p k c", p=128), w=[("wq", s)])
                    for half in range(2):
                        fo = fo2 + half
                        pi = (2 * pidx + half) % 4
                        self.mm_fm(ps[pi], ("ps", pi), wq[s], ("wq", s), KC, h, lambda k: ("h", k), Tt, wcol0=half * 128)
                        qs = qk[fo % 2]
                        sc = 0.125 if fo < 16 else 1.0
                        S.op("act", (lambda e, qs=qs, pi=pi, sc=sc: e.activation(
                            out=qs[:, :Tt], in_=ps[pi][:, :Tt], func=AF.Copy, scale=sc)),
                            r=[("ps", pi)], w=[("qk", fo % 2)])
                        if fo < 16:
                            dst = self.QT[fo * 128:(fo + 1) * 128, t0:t1]
                        elif ctx:
                            dst = self.KcT[(fo - 16) * 128:(fo - 15) * 128, t0 - CPAD:t1 - CPAD]
                        else:
                            dst = self.KT[(fo - 16) * 128:(fo - 15) * 128, t0:t1]
                        S.dma("sp", dst, qs[:, :Tt], r=[("qk", fo % 2)], w=[("qkd", fo)])
                ntb = Tt // 128
                for cg in range(4):
                    s = cg % 2
                    S.dma("pool", wv[s][:, :, :],
                          self.w_qkv[:, 2 * D + cg * 512:2 * D + (cg + 1) * 512].rearrange("(k p) c -> p k c", p=128),
                          w=[("wv", s)])
                    for tb in range(ntb):
                        u = cg * ntb + tb
                        pi = u % 4
                        for k in range(KC):
                            S.op("pe", (lambda e, pi=pi, k=k, tb=tb, s=s: e.matmul(
                                ps[pi][:, 0:512], lhsT=h[:, k, tb * 128:(tb + 1) * 128], rhs=wv[s][:, k, :],
                                start=(k == 0), stop=(k == KC - 1))), r=[("h", k), ("wv", s)], w=[("ps", pi)])
                        vs = vst[u % 2]
                        S.op("dve", (lambda e, vs=vs, pi=pi: e.tensor_copy(
                            out=vs[:, :, 0:64], in_=ps[pi][:, 0:512].rearrange("p (h d) -> p h d", d=64))),
                            r=[("ps", pi)], w=[("vst", u % 2)])
                        if ctx:
                            dst = self.VcA[4 * cg:4 * cg + 4, t0 - CPAD + tb * 128:t0 - CPAD + (tb + 1) * 128, :]
                        else:
                            dst = self.VA[4 * cg:4 * cg + 4, t0 + tb * 128:t0 + (tb + 1) * 128, :]
                        S.dma("sp", dst.rearrange("a t c -> t a c"),
                              vs[:, :, :].rearrange("p (a b) c -> p a (b c)", b=2), r=[("vst", u % 2)], w=[("vd", u)])
            for (t0, t1) in tiles:
                body(t0, t1)
            S.flush()

    def phase_att(self):
        S, nc = self.S, self.nc
        NBLK = TE // 128
        with ExitStack() as st:
            ksb = [self.sb(st, "ksb%d" % i, [128, TE], BF16) for i in range(2)]
            qsb = [self.sb(st, "qsb%d" % i, [128, TE], BF16) for i in range(2)]
            vsb = [self.sb(st, "vsb%d" % i, [128, NBLK, 130], BF16) for i in range(2)]
            kcs = [self.sb(st, "kcs%d" % i, [128, CTXL], BF16) for i in range(2)]
            vcs = [self.sb(st, "vcs%d" % i, [128, 2, 130], BF16) for i in range(2)]
            bt = [self.sb(st, "bt%d" % i, [128, 2, NTAB, 128], F32) for i in range(2)]
            ssb = [self.sb(st, "ssb%d" % i, [128, 6, 128], F32) for i in range(2)]
            pT = [self.sb(st, "pT%d" % i, [128, 1024], BF16) for i in range(2)]
            osb = [self.sb(st, "osb%d" % i, [128, 128], BF16) for i in range(2)]
            rden = [self.sb(st, "rden%d" % i, [128, 2], F32) for i in range(2)]
            otsb = [self.sb(st, "otsb%d" % i, [128, NQB * 128], BF16) for i in range(2)]
            ps_s = [self.ps(st, "pss%d" % i, [128, 1024]) for i in range(2)]
            ps_o = [self.ps(st, "pso%d" % i, [128, 512]) for i in range(2)]
            ps_t = [self.ps(st, "pst%d" % i, [128, 128], BF16) for i in range(2)]
            def loads(pr):
                s = pr % 2
                rows = slice(pr * 128, (pr + 1) * 128)
                S.dma("sp", ksb[s][:, :], self.KT[rows, :], w=[("k", s)])
                S.dma("sp", qsb[s][:, :], self.QT[rows, :], w=[("q", s)])
                S.dma("sp", vsb[s][:, :, :], self.VA[pr].rearrange("(b p) c -> p b c", p=128), w=[("v", s)])
                S.dma("sp", kcs[s][:, :], self.KcT[rows, :], w=[("kc", s)])
                S.dma("sp", vcs[s][:, :, :], self.VcA[pr].rearrange("(b p) c -> p b c", p=128), w=[("vc", s)])
                for hh in range(2):
                    S.dma("sp", bt[s][:, hh, :, :], self.btab[2 * pr + hh].rearrange("j k q -> k j q"), w=[("bt", s, hh)])

            def unit_info(n):
                pr, rem = divmod(n, NQB * 2)
                qb, hh = divmod(rem, 2)
                qbi = QB0 + qb
                if qbi == 3:
                    cls, kbs = "A", list(range(qbi - 2, qbi + 4))
                elif qbi == 4:
                    cls, kbs = "B", list(range(qbi - 2, qbi + 3))
                elif qbi == 33:
                    cls, kbs = "C", list(range(qbi - 2, qbi + 3))
                elif qbi == 34:
                    cls, kbs = "D", list(range(qbi - 3, qbi + 3))
                else:
                    cls, kbs = "I", list(range(qbi - 2, qbi + 3))
                return pr, qb, hh, qbi, TAB_OFF[cls], kbs

            def front(n):
                pr, qb, hh, qbi, off, kbs = unit_info(n)
                s = pr % 2
                u = n % 2
                nbk = len(kbs)
                hp = slice(hh * 64, (hh + 1) * 64)
                pss = ps_s[u]
                for i, kb in enumerate(kbs):
                    S.op("pe", (lambda e, i=i, kb=kb: e.matmul(
                        pss[:, i * 128:(i + 1) * 128], lhsT=ksb[s][hp, kb * 128:(kb + 1) * 128],
                        rhs=qsb[s][hp, qbi * 128:(qbi + 1) * 128], start=True, stop=True)),
                        r=[("k", s), ("q", s)], w=[("pss", u)])
                for cbk in range(2):
                    S.op("pe", (lambda e, cbk=cbk: e.matmul(
                        pss[:, (nbk + cbk) * 128:(nbk + cbk + 1) * 128], lhsT=kcs[s][hp, cbk * 128:(cbk + 1) * 128],
                        rhs=qsb[s][hp, qbi * 128:(qbi + 1) * 128], start=True, stop=True)),
                        r=[("kc", s), ("q", s)], w=[("pss", u)])
                S.op("dve", (lambda e: e.tensor_tensor(
                    out=ssb[u][:, 0:nbk, :], in0=pss[:, 0:nbk * 128].rearrange("p (j q) -> p j q", q=128),
                    in1=bt[s][:, hh, off:off + nbk, :], op=ALU.add)),
                    r=[("pss", u), ("bt", s, hh)], w=[("ssb", u)])
                S.op("act", (lambda e: e.activation(
                    out=pT[u][:, 0:nbk * 128].rearrange("p (j q) -> p j q", q=128), in_=ssb[u][:, 0:nbk, :],
                    func=AF.Exp)), r=[("ssb", u)], w=[("pT", u)])
                S.op("act", (lambda e: e.activation(
                    out=pT[u][:, nbk * 128:(nbk + 2) * 128], in_=pss[:, nbk * 128:(nbk + 2) * 128],
                    func=AF.Exp)), r=[("pss", u)], w=[("pT", u)])

            def back(n):
                pr, qb, hh, qbi, off, kbs = unit_info(n)
                s = pr % 2
                u = n % 2
                nbk = len(kbs)
                hp = slice(hh * 64, (hh + 1) * 64)
                ob = qb % 2
                pso = ps_o[u]
                for i, kb in enumerate(kbs):
                    S.op("pe", (lambda e, i=i, kb=kb: e.matmul(
                        pso[:, 0:65], lhsT=pT[u][:, i * 128:(i + 1) * 128], rhs=vsb[s][:, kb, hh * 65:(hh + 1) * 65],
                        start=(i == 0), stop=False)), r=[("pT", u), ("v", s)], w=[("pso", u)])
                for cbk in range(2):
                    S.op("pe", (lambda e, cbk=cbk: e.matmul(
                        pso[:, 0:65], lhsT=pT[u][:, (nbk + cbk) * 128:(nbk + cbk + 1) * 128],
                        rhs=vcs[s][:, cbk, hh * 65:(hh + 1) * 65], start=False, stop=(cbk == 1))),
                        r=[("pT", u), ("vc", s)], w=[("pso", u)])
                S.op("dve", (lambda e: e.reciprocal(out=rden[u][:, 0:1], in_=pso[:, 64:65])),
                     r=[("pso", u)], w=[("rden", u)])
                S.op("act", (lambda e: e.activation(
                    out=osb[ob][:, hp], in_=pso[:, 0:64], func=AF.Copy, scale=rden[u][:, 0:1])),
                    r=[("pso", u), ("rden", u)], w=[("osb", ob)])
                if hh == 1:
                    S.op("pe", (lambda e: e.transpose(out=ps_t[ob][:, :], in_=osb[ob][:, :], identity=self.ident[:, :])),
                         r=[("osb", ob), "ident"], w=[("pst", ob)])
                    S.op("dve", (lambda e: e.tensor_copy(out=otsb[s][:, qb * 128:(qb + 1) * 128], in_=ps_t[ob][:, :])),
                         r=[("pst", ob)], w=[("otsb", s)])
                    if qb == NQB - 1:
                        rows = slice(pr * 128, (pr + 1) * 128)
                        S.dma("sp", self.OT[rows, QB0 * 128:(QB0 + NQB) * 128], otsb[s][:, :], r=[("otsb", s)],
                              w=[("otd", pr)])

            NU = NPAIR * NQB * 2
            loads(0)
            front(0)
            for n in range(NU):
                if n % (NQB * 2) == 0 and n // (NQB * 2) + 1 < NPAIR:
                    loads(n // (NQB * 2) + 1)
                if n + 1 < NU:
                    front(n + 1)
                back(n)
            S.flush()

    def phase_wo(self, l, a, b):
        S = self.S
        with ExitStack() as st:
            tmax = 1024
            ob = self.sb(st, "ob", [128, KC, tmax], BF16)
            wo = [self.sb(st, "wo%d" % i, [128, KC, 256], BF16) for i in range(2)]
            xe = [self.sb(st, "xe%d" % i, [128, tmax], F32) for i in range(2)]
            xo = [self.sb(st, "xo%d" % i, [128, tmax], F32) for i in range(2)]
            ps = [self.ps(st, "pw%d" % i, [128, 1024]) for i in range(4)]

            def body(t0, t1):
                Tt = t1 - t0
                S.dma("sp", ob[:, :, :Tt], self.OT[:, t0:t1].rearrange("(k p) t -> p k t", p=128), w=["ob"])
                for e2 in range(8):
                    s = e2 % 2
                    S.dma("pool", wo[s][:, :, :], self.w_o[:, e2 * 256:(e2 + 1) * 256].rearrange("(k p) c -> p k c", p=128),
                          w=[("wo", s)])
                    for half in range(2):
                        ec = 2 * e2 + half
                        pi = ec % 4
                        xs = ec % 2
                        self.mm_fm(ps[pi], ("ps", pi), wo[s], ("wo", s), KC, ob, lambda k: "ob", Tt, wcol0=half * 128)
                        S.dma("sp", xe[xs][:, :Tt], self.xa[ec * 128:(ec + 1) * 128, t0:t1], w=[("xe", xs)])
                        S.op("dve", (lambda e, pi=pi, xs=xs, ec=ec: e.scalar_tensor_tensor(
                            out=xo[xs][:, :Tt], in0=ps[pi][:, :Tt], scalar=self.mod(l, 0, 2, ec), in1=xe[xs][:, :Tt],
                            op0=ALU.mult, op1=ALU.add)), r=[("ps", pi), ("M", l), ("xe", xs)], w=[("xo", xs)])
                        S.dma("sp", self.xb[ec * 128:(ec + 1) * 128, t0:t1], xo[xs][:, :Tt], r=[("xo", xs)], w=[("dst", ec)])
            for (t0, t1) in _tiles(a, b, tmax):
                body(t0, t1)
            S.flush()

    def phase_convmod(self, l, src, dst, a, b):
        S = self.S
        with ExitStack() as st:
            tmax = 480
            wmax = tmax + 30
            nb = self.norm_bufs(st, wmax, with_vm=True, gs=2)
            h = self.sb(st, "hc", [128, KC, wmax], BF16)
            w1 = [self.sb(st, "wc1%d" % i, [128, KC, 256], BF16) for i in range(2)]
            w2 = [self.sb(st, "wc2%d" % i, [128, KC, 256], BF16) for i in range(2)]
            sig = [self.sb(st, "sig%d" % i, [128, wmax], F32) for i in range(2)]
            uu = [self.sb(st, "uu%d" % i, [128, wmax], F32) for i in range(2)]
            y = self.sb(st, "yc", [128, KC, tmax], F32)
            yb = [self.sb(st, "yb%d" % i, [128, tmax], BF16) for i in range(2)]
            ysq = [self.sb(st, "ysq%d" % i, [128, tmax], BF16) for i in range(2)]
            mean = self.sb(st, "mean", [128, tmax], F32)
            rs = self.sb(st, "rs", [128, tmax], F32)
            tn = [self.sb(st, "tn%d" % i, [128, tmax], F32) for i in range(2)]
            z = self.sb(st, "zc", [128, KC, tmax], BF16)
            xe = [self.sb(st, "cxe%d" % i, [128, tmax], F32) for i in range(2)]
            xo = [self.sb(st, "cxo%d" % i, [128, tmax], F32) for i in range(2)]
            gb2 = self.sb(st, "gb2", [128, KC], F32)
            NPE = 12
            TP0 = 31 - NPE
            dgs = self.sb(st, "dgs", [128, KC * NPE, 128], BF16)
            ub = [self.sb(st, "ub%d" % i, [128, wmax], BF16) for i in range(2)]
            ps = [self.ps(st, "pc%d" % i, [128, 512]) for i in range(8)]
            cvs = self.cvs
            for c in range(KC):
                for i in range(NPE):
                    S.op("dve", (lambda e, c=c, i=i: e.tensor_scalar_mul(
                        out=dgs[:, c * NPE + i, :], in0=self.ident[:, :], scalar1=cvs[:, c, TP0 + i:TP0 + i + 1])),
                        r=["ident", "cvs"], w=["dgs"])
            S.op("dve", lambda e: e.tensor_tensor(out=gb2[:, :], in0=self.M[l][:, 0, 2 * KC:3 * KC], in1=cvs[:, :, 36],
                                                  op=ALU.mult), r=[("M", l), "cvs"], w=["gb2"])

            def body(t0, t1):
                Tt = t1 - t0
                c0, c1 = t0 - 15, t1 + 15
                W = c1 - c0
                masked = c0 < O0 or c1 > E0
                self.norm_stats(src, c0, c1, nb, ps[0], ("ps", 0))
                self.ada_h(src, c0, c1, nb, l, 0, 0, h, False)
                if masked:
                    S.dma("sp", nb["vm"][:, :W], self.vmask[:, c0:c1], w=["vm"])
                def glu(c):
                    s = c % 2
                    for half, col in enumerate((c * 128, D + c * 128)):
                        S.dma("pool", w1[s][:, :, half * 128:(half + 1) * 128],
                              self.w_pw1[:, col:col + 128].rearrange("(k p) c -> p k c", p=128), w=[("w1", s, half)])
                    pa, pg = (2 * c) % 4, (2 * c + 1) % 4
                    self.mm_fm(ps[pa], ("ps", pa), w1[s], ("w1", s, 0), KC, h, lambda k: ("h", k), W, wcol0=0)
                    self.mm_fm(ps[pg], ("ps", pg), w1[s], ("w1", s, 1), KC, h, lambda k: ("h", k), W, wcol0=128)
                    S.op("act", (lambda e: e.activation(
                        out=sig[s][:, :W], in_=ps[pg][:, :W], func=AF.Sigmoid, bias=cvs[:, c, 35:36])),
                        r=[("ps", pg), "cvs"], w=[("sig", s)])

                def glu2(c):
                    s = c % 2
                    pa = (2 * c) % 4
                    S.op("dve", (lambda e: e.scalar_tensor_tensor(
                        out=uu[s][:, :W], in0=ps[pa][:, :W], scalar=cvs[:, c, 34:35], in1=sig[s][:, :W],
                        op0=ALU.add, op1=ALU.mult)), r=[("ps", pa), "cvs", ("sig", s)], w=[("uu", s)])
                    if masked:
                        S.op("dve", (lambda e: e.tensor_tensor(
                            out=uu[s][:, :W], in0=uu[s][:, :W], in1=nb["vm"][:, :W], op=ALU.mult)),
                            r=[("uu", s), "vm"], w=[("uu", s)])
                    S.op("act", (lambda e: e.copy(out=ub[s][:, :W], in_=uu[s][:, :W])), r=[("uu", s)], w=[("ub", s)])

                glu(0)
                glu2(0)
                for c in range(KC):
                    s = c % 2
                    pcv = 6 + c % 2
                    for i in range(NPE):
                        S.op("pe", (lambda e, s=s, c=c, i=i, pcv=pcv: e.matmul(
                            ps[pcv][:, :Tt], lhsT=dgs[:, c * NPE + i, :], rhs=ub[s][:, TP0 + i:TP0 + i + Tt],
                            start=(i == 0), stop=(i == NPE - 1))), r=["dgs", ("ub", s)], w=[("ps", pcv)])
                    if c + 1 < KC:
                        glu(c + 1)
                    en = "dve"
                    S.op(en, (lambda e, s=s, c=c: e.tensor_scalar(
                        out=y[:, c, :Tt], in0=uu[s][:, 0:Tt], scalar1=cvs[:, c, 0:1], scalar2=cvs[:, c, 31:32],
                        op0=ALU.mult, op1=ALU.add)), r=[("uu", s), "cvs"], w=[("y", c)])
                    for tp in range(1, TP0):
                        S.op(en, (lambda e, s=s, c=c, tp=tp: e.scalar_tensor_tensor(
                            out=y[:, c, :Tt], in0=uu[s][:, tp:tp + Tt], scalar=cvs[:, c, tp:tp + 1], in1=y[:, c, :Tt],
                            op0=ALU.mult, op1=ALU.add)), r=[("uu", s), "cvs", ("y", c)], w=[("y", c)])
                    S.op(en, (lambda e, c=c, pcv=pcv: e.tensor_tensor(
                        out=y[:, c, :Tt], in0=y[:, c, :Tt], in1=ps[pcv][:, :Tt], op=ALU.add)),
                        r=[("y", c), ("ps", pcv)], w=[("y", c)])
                    if c + 1 < KC:
                        glu2(c + 1)
                    S.op("act", (lambda e, s=s, c=c: e.copy(out=yb[s][:, :Tt], in_=y[:, c, :Tt])),
                         r=[("y", c)], w=[("yb", s)])
                    S.op("act", (lambda e, s=s, c=c: e.activation(out=ysq[s][:, :Tt], in_=y[:, c, :Tt], func=AF.Square)),
                         r=[("y", c)], w=[("ysq", s)])
                    S.op("pe", (lambda e, s=s, c=c: e.matmul(ps[4][:, :Tt], lhsT=self.ones[:, :], rhs=yb[s][:, :Tt],
                                                             start=(c == 0), stop=(c == KC - 1))),
                         r=[("yb", s), "ones"], w=[("ps", 4)])
                    S.op("pe", (lambda e, s=s, c=c: e.matmul(ps[5][:, :Tt], lhsT=self.ones[:, :], rhs=ysq[s][:, :Tt],
                                                             start=(c == 0), stop=(c == KC - 1))),
                         r=[("ysq", s), "ones"], w=[("ps", 5)])
                S.op("dve", (lambda e: e.tensor_scalar_mul(out=mean[:, :Tt], in0=ps[4][:, :Tt], scalar1=1.0 / D)),
                     r=[("ps", 4)], w=["mean"])
                S.op("dve", (lambda e: e.tensor_tensor(out=rs[:, :Tt], in0=mean[:, :Tt], in1=mean[:, :Tt], op=ALU.mult)),
                     r=["mean"], w=["rs"])
                S.op("dve", (lambda e: e.scalar_tensor_tensor(out=rs[:, :Tt], in0=ps[5][:, :Tt], scalar=1.0 / D,
                                                              in1=rs[:, :Tt], op0=ALU.mult, op1=ALU.subtract)),
                     r=[("ps", 5), "rs"], w=["rs"])
                S.op("dve", (lambda e: e.tensor_scalar_add(out=rs[:, :Tt], in0=rs[:, :Tt], scalar1=LN_EPS)),
                     r=["rs"], w=["rs"])
                S.op("act", (lambda e: e.activation(out=rs[:, :Tt], in_=rs[:, :Tt], func=AF.Sqrt)), r=["rs"], w=["rs"])
                S.op("dve", (lambda e: e.reciprocal(out=rs[:, :Tt], in_=rs[:, :Tt])), r=["rs"], w=["rs"])
                for c in range(KC):
                    s = c % 2
                    S.op("dve", (lambda e, s=s, c=c: e.tensor_tensor(
                        out=tn[s][:, :Tt], in0=y[:, c, :Tt], in1=mean[:, :Tt], op=ALU.subtract)),
                        r=[("y", c), "mean"], w=[("tn", s)])
                    S.op("dve", (lambda e, s=s: e.tensor_tensor(
                        out=tn[s][:, :Tt], in0=tn[s][:, :Tt], in1=rs[:, :Tt], op=ALU.mult)),
                        r=[("tn", s), "rs"], w=[("tn", s)])
                    S.op("act", (lambda e, s=s, c=c: e.activation(
                        out=z[:, c, :Tt], in_=tn[s][:, :Tt], func=AF.Silu, bias=cvs[:, c, 33:34], scale=cvs[:, c, 32:33])),
                        r=[("tn", s), "cvs"], w=[("z", c)])
                for e2 in range(8):
                    s = e2 % 2
                    S.dma("pool", w2[s][:, :, :], self.w_pw2[:, e2 * 256:(e2 + 1) * 256].rearrange("(k p) c -> p k c", p=128),
                          w=[("w2", s)])
                    for half in range(2):
                        ec = 2 * e2 + half
                        pi = 6 + ec % 2
                        xs = ec % 2
                        self.mm_fm(ps[pi], ("ps", pi), w2[s], ("w2", s), KC, z, lambda k: ("z", k), Tt, wcol0=half * 128)
                        S.dma("sp", xe[xs][:, :Tt], src[ec * 128:(ec + 1) * 128, t0:t1], w=[("xe", xs)])
                        S.op("dve", (lambda e, pi=pi, xs=xs, ec=ec: e.scalar_tensor_tensor(
                            out=xo[xs][:, :Tt], in0=ps[pi][:, :Tt], scalar=self.mod(l, 0, 2, ec), in1=xe[xs][:, :Tt],
                            op0=ALU.mult, op1=ALU.add)), r=[("ps", pi), ("M", l), ("xe", xs)], w=[("xo", xs)])
                        S.op("dve", (lambda e, xs=xs, ec=ec: e.tensor_scalar_add(
                            out=xo[xs][:, :Tt], in0=xo[xs][:, :Tt], scalar1=gb2[:, ec:ec + 1])),
                            r=[("xo", xs), "gb2"], w=[("xo", xs)])
                        S.dma("sp", dst[ec * 128:(ec + 1) * 128, t0:t1], xo[xs][:, :Tt], r=[("xo", xs)], w=[("dst", ec)])
            for (t0, t1) in _tiles(a, b, tmax):
                body(t0, t1)
            S.flush()

    def phase_final(self, src):
        S = self.S
        with ExitStack() as st:
            tmax = 1024
            nb = self.norm_bufs(st, tmax, with_vm=False)
            ps = self.ps(st, "pfin", [128, 1024])
            xo = [self.sb(st, "fo%d" % i, [128, 2, tmax], F32) for i in range(2)]
            xg = nb["xg"]
            def body(t0, t1):
                W = t1 - t0
                self.norm_stats(src, t0, t1, nb, ps, "psf")
                for g in range(8):
                    s = g % 2
                    S.dma("sp", xg[s][:, :, :W], src[g * 256:(g + 1) * 256, t0:t1].rearrange("(k p) t -> p k t", p=128),
                          w=[("xg", s)])
                    for kk in range(2):
                        k = 2 * g + kk
                        S.op("dve", (lambda e, s=s, kk=kk, k=k: e.scalar_tensor_tensor(
                            out=xo[s][:, kk, :W], in0=xg[s][:, kk, :W], scalar=self.fngs[:, k:k + 1],
                            in1=nb["rstd"][:, :W], op0=ALU.mult, op1=ALU.mult)),
                            r=[("xg", s), "rstd", "fngs"], w=[("fo", s)])
                    S.dma("sp", self.outT[g * 256:(g + 1) * 256, t0 - O0:t1 - O0].rearrange("(k p) t -> p k t", p=128),
                          xo[s][:, :, :W], r=[("fo", s)], w=[("out", g)])
            for (t0, t1) in _tiles(O0, E0, tmax):
                body(t0, t1)
            S.flush()

    def build(self):
        S, nc = self.S, self.nc
        with ExitStack() as st0:
            self.consts(st0)
            self.phase_mod()
            S.dma("sp", self.xa[:, 0:64], self.xin[:, 0:64], w=["xa_lo"])
            S.dma("sp", self.xa[:, TE - 128:TE], self.xin[:, TE - 128:TE], w=["xa_hi"])
            inval = (O0, E0)
            inval_c = (CPAD, CPAD + CTXL)
            nl = self.n_layers
            cur = None
            if nl >= 1:
                self.phase_pool(0, 0, 0, self.xin, self.xb, 63, 4737, 936, inval, self.vmask, self.rcnt)
                self.phase_ffn(0, 0, self.xb, self.xa, 64, 4736, 936, inval, self.vmask)
                self.phase_pool(0, 0, 1, self.cin, self.cb, 15, 273, 936, inval_c, self.vmask_c, self.rcnt_c)
                self.phase_ffn(0, 1, self.cb, self.ca, 16, 272, 936, inval_c, self.vmask_c)
            if nl >= 2:
                self.phase_qkv(0, self.xa, [(i * 1024, min((i + 1) * 1024, TE)) for i in range(5)], False)
                self.phase_qkv(1, self.ca, [(CPAD, CPAD + CTXL)], True)
                self.phase_att()
                self.phase_wo(1, QB0 * 128, (QB0 + NQB) * 128)
                self.phase_ffn(1, 0, self.xb, self.xa, 352, 4512, 936, inval, self.vmask)
            if nl >= 3:
                self.phase_convmod(2, self.xa, self.xb, 371, 4493)
                self.phase_ffn(2, 0, self.xb, self.xa, 372, 4492, 936, inval, self.vmask)
            if nl >= 4:
                self.phase_pool(3, 1, 0, self.xa, self.xb, 383, 4481, 936, inval, self.vmask, self.rcnt)
                self.phase_ffn(3, 0, self.xb, self.xa, 384, 4480, 936, inval, self.vmask)
            if self.dbg:
                S.flush()
                for name, do in self.dbg_outs.items():
                    dsrc = getattr(self, name)
                    n0 = dsrc.shape[0]
                    step = 128 if len(dsrc.shape) == 2 else 1
                    for k in range(0, n0, step):
                        S.dma("sp", do[k:k + step], dsrc[k:k + step], w=[("dbgo", name, k)])
            if nl >= 4:
                self.phase_final(self.xa)
            S.finish()
        return nc


def _chunked(v):
    return np.ascontiguousarray(v.reshape(-1, 128).T)


def _core_inputs(core, inp, shared):
    b, q = divmod(core, 4)
    g0 = q * OWN - HALO
    lo, hi = max(g0, 0), min(g0 + TE, SEQ)
    xT = np.zeros((D, TE), np.float32)
    xT[:, lo - g0:hi - g0] = inp["x"][b, lo:hi, :].T
    valid = np.zeros(TE, np.float32)
    valid[lo - g0:hi - g0] = 1.0
    cT = np.zeros((D, TEC), np.float32)
    cT[:, CPAD:CPAD + CTXL] = inp["ctx"][b].T
    cs = np.stack([_chunked(inp["c"][b]), _chunked(inp["c_ctx"])], axis=-1).astype(np.float32)
    m = dict(shared)
    m.update(xin=xT, cin=cT, vmask=np.ascontiguousarray(np.broadcast_to(valid, (128, TE))),
             rcnt=_rcnt(valid), cs=np.ascontiguousarray(cs))
    m["btab"] = shared["_btab"][0 if q == 0 else (2 if q == 3 else 1)]
    del m["_btab"]
    return m


def _rcnt(valid):
    n = valid.shape[0]
    cs = np.concatenate([[0.0], np.cumsum(valid)])
    t = np.arange(n)
    out = np.zeros((4, n), np.float32)
    for g, w in enumerate((2, 4, 8, 16)):
        lo = np.clip(t - w // 2, 0, n)
        hi = np.clip(t - w // 2 + w, 0, n)
        cnt = cs[hi] - cs[lo]
        out[g] = np.where(cnt > 0, 1.0 / np.maximum(cnt, 1.0), 0.0)
    return np.ascontiguousarray(np.broadcast_to(out, (128, 4, n))).astype(np.float32)


def _bias_tables(rpb):
    k = np.arange(128)
    kr, kc = k // 64, k % 64
    qr, qc = kr, kc
    cstart = np.clip(qc - 8, 0, GRID_W - 16)
    cval = (kc[:, None] >= cstart[None, :]) & (kc[:, None] < cstart[None, :] + 16)
    coff = np.clip(kc[:, None] - qc[None, :], -15, 15) + 15

    def tile(grow_q0, koff):
        gq = grow_q0 + qr[None, :]
        gk = grow_q0 + 2 * koff + kr[:, None]
        r0 = np.clip(gq - 4, 0, ROWS - 8)
        ok = cval & (gk >= r0) & (gk < r0 + 8) & (gk >= 0) & (gk < ROWS)
        roff = np.clip(gk - gq + 7, 0, 14)
        vals = rpb[:, roff, coff]
        return np.where(ok[None], vals, NEG).astype(np.float32)

    def tset(rowA, rowB, rowC, rowD):
        tl = []
        for off in range(-2, 3):
            tl.append(tile(100, off))
        for off in range(-2, 4):
            tl.append(tile(rowA, off))
        for off in range(-2, 3):
            tl.append(tile(rowB, off))
        for off in range(-2, 3):
            tl.append(tile(rowC, off))
        for off in range(-3, 3):
            tl.append(tile(rowD, off))
        return np.ascontiguousarray(np.stack(tl, axis=1))

    first = tset(0, 2, 100, 100)
    inter = tset(100, 100, 100, 100)
    last = tset(100, 100, ROWS - 4, ROWS - 2)
    return [first, inter, last]


def _shared_inputs(inp):
    f = np.float32
    sh = {}
    sh["w_mod"] = np.ascontiguousarray(inp["w_mod"], dtype=f)
    sh["bmod"] = np.ascontiguousarray(np.stack([_chunked(inp["b_mod"][l]) for l in range(4)]), dtype=f)
    sh["pool_w"] = np.ascontiguousarray(inp["pool_w"], dtype=f)
    sh["pool_ls"] = np.ascontiguousarray(np.stack([_chunked(inp["pool_scale"][j]) for j in range(2)]), dtype=f)
    sh["w_qkv"] = np.ascontiguousarray(inp["na_w_qkv"][0], dtype=f)
    sh["w_o"] = np.ascontiguousarray(inp["na_w_o"][0], dtype=f)
    sh["_btab"] = _bias_tables(np.asarray(inp["na_rpb"][0], dtype=f))
    sh["w_pw1"] = np.ascontiguousarray(inp["cv_w_pw1"][0], dtype=f)
    sh["w_pw2"] = np.ascontiguousarray(inp["cv_w_pw2"][0], dtype=f)
    cvv = np.zeros((128, KC, 40), f)
    for t in range(31):
        cvv[:, :, t] = _chunked(inp["cv_w_dw"][0][t])
    cvv[:, :, 31] = _chunked(inp["cv_b_dw"][0])
    cvv[:, :, 32] = _chunked(inp["cv_ln_g"][0])
    cvv[:, :, 33] = _chunked(inp["cv_ln_b"][0])
    cvv[:, :, 34] = _chunked(inp["cv_b_pw1"][0][:D])
    cvv[:, :, 35] = _chunked(inp["cv_b_pw1"][0][D:])
    cvv[:, :, 36] = _chunked(inp["cv_b_pw2"][0])
    sh["cvv"] = cvv
    sh["w_up"] = np.ascontiguousarray(inp["ffn_w_up"], dtype=f)
    sh["w_dn"] = np.ascontiguousarray(inp["ffn_w_down"], dtype=f)
    cw = np.zeros((128, 4, 2 * FC, 4), f)
    for l in range(4):
        for t in range(3):
            cw[:, l, :, t] = _chunked(inp["ffn_w_dw"][l][t])
        cw[:, l, :, 3] = _chunked(inp["ffn_b_dw"][l])
    sh["cw"] = cw
    sh["fng"] = _chunked(np.asarray(inp["final_norm_g"], dtype=f))
    sh["ident_in"] = np.eye(128, dtype=f)
    vc = np.zeros(TEC, f)
    vc[CPAD:CPAD + CTXL] = 1.0
    sh["vmask_c"] = np.ascontiguousarray(np.broadcast_to(vc, (128, TEC)))
    sh["rcnt_c"] = _rcnt(vc)
    return sh


def kernel(**inputs):
    inp = {k: np.asarray(v) for k, v in inputs.items()}
    shared = _shared_inputs(inp)
    nc = Builder().build()
    in_maps = [_core_inputs(c, inp, shared) for c in range(NCORE)]
    res = run_bass_kernel_spmd(nc, in_maps, core_ids=list(range(NCORE)))
    out = np.empty((NB, SEQ, D), np.float32)
    for c in range(NCORE):
        b, q = divmod(c, 4)
        out[b, q * OWN:(q + 1) * OWN, :] = res.results[c]["outT"].T
    return out
```

```python
import numpy as np
from contextlib import ExitStack
import concourse.bass as bass
import concourse.mybir as mybir
from concourse.bass_utils import run_bass_kernel_spmd

F32 = mybir.dt.float32
BF16 = mybir.dt.bfloat16
AF = mybir.ActivationFunctionType
ALU = mybir.AluOpType

D = 2048
KC = 16
DFF = 5632
FC = 44
SEQ = 16384
NB = 2
NCORE = 8
OWN = 4096
HALO = 384
TE = OWN + 2 * HALO
O0 = HALO
E0 = HALO + OWN
CTXL = 256
CPAD = 16
TEC = CTXL + 2 * CPAD
GRID_W = 64
ROWS = SEQ // GRID_W
NHEAD = 32
NPAIR = 16
EPS = 1e-6
LN_EPS = 1e-5
NEG = -30000.0
NQB = 34
QB0 = 2
NTAB = 27
TAB_OFF = {"I": 0, "A": 5, "B": 11, "C": 16, "D": 21}


class _Op:
    __slots__ = ("stream", "fn", "dma", "deps", "signal", "sig", "slot", "semval", "prev")

    def __init__(self, stream, fn, dma):
        self.stream = stream
        self.fn = fn
        self.dma = dma
        self.deps = set()
        self.signal = False
        self.sig = 0
        self.slot = None
        self.semval = 0
        self.prev = None


class Sched:
    STREAMS = ("pe", "act", "dve", "pool", "sp")
    NSLOT = {"sp": 12, "pool": 8}

    def __init__(self, nc):
        self.nc = nc
        self.eng = {"pe": nc.tensor, "act": nc.scalar, "dve": nc.vector, "pool": nc.gpsimd, "sp": nc.sync}
        self.batch = []
        self.last_w = {}
        self.readers = {}
        self.cnt = {s: 0 for s in self.STREAMS}
        self.dcnt = {s: 0 for s in self.STREAMS}
        self.slots = {}
        self.sem_c = {s: nc.alloc_semaphore("c_" + s) for s in self.STREAMS}
        self.sem_d = {}
        self.waited = {s: {} for s in self.STREAMS}
        self.bar_deps = []
        self.bar_pending = {s: False for s in self.STREAMS}
        self.last_op = {s: None for s in self.STREAMS}
        self.n_ops = 0
        self.n_waits = 0

    def op(self, stream, fn, r=(), w=(), dma=False):
        o = _Op(stream, fn, dma)
        deps = o.deps
        lw = self.last_w
        rd = self.readers
        for k in r:
            d = lw.get(k)
            if d is not None:
                deps.add(d)
        for k in w:
            d = lw.get(k)
            if d is not None:
                deps.add(d)
            x = rd.get(k)
            if x:
                deps.update(x.values())
        rkey = (stream, len(self.batch)) if dma else stream
        for k in r:
            x = rd.get(k)
            if x is None:
                x = rd[k] = {}
            x[rkey] = o
        for k in w:
            lw[k] = o
            rd[k] = {}
        deps.discard(o)
        if self.bar_pending[stream]:
            deps.update(self.bar_deps)
            self.bar_pending[stream] = False
        self.batch.append(o)
        return o

    def dma(self, stream, out, in_, r=(), w=()):
        return self.op(stream, lambda e: e.dma_start(out=out, in_=in_), r, w, dma=True)

    def flush(self):
        for o in self.batch:
            for d in o.deps:
                if (not d.dma) and d.stream != o.stream:
                    d.signal = True
        last = {}
        for o in self.batch:
            if not o.dma:
                last[o.stream] = o
        for o in last.values():
            o.signal = True
        for o in self.batch:
            if o.dma:
                n = self.NSLOT[o.stream]
                k = self.dcnt[o.stream]
                self.dcnt[o.stream] += 1
                key = (o.stream, k % n)
                st = self.slots.get(key)
                if st is None:
                    st = self.slots[key] = [None, 0]
                    self.sem_d[key] = self.nc.alloc_semaphore("d_%s_%d" % key)
                o.prev = st[0]
                st[1] += 16
                o.slot = key
                o.semval = st[1]
                st[0] = o
            else:
                if o.signal:
                    self.cnt[o.stream] += 1
                    o.sig = self.cnt[o.stream]
                self.last_op[o.stream] = o
        for o in self.batch:
            e = self.eng[o.stream]
            wt = self.waited[o.stream]
            need = {}
            deps = list(o.deps)
            if o.dma and o.prev is not None:
                deps.append(o.prev)
            for d in deps:
                if d.dma:
                    key = ("d", d.slot)
                    v = d.semval
                elif d.stream != o.stream:
                    key = ("c", d.stream)
                    v = d.sig
                else:
                    continue
                if need.get(key, 0) < v:
                    need[key] = v
            for key, v in need.items():
                if wt.get(key, 0) >= v:
                    continue
                wt[key] = v
                sem = self.sem_d[key[1]] if key[0] == "d" else self.sem_c[key[1]]
                e.wait_ge(sem, v)
                self.n_waits += 1
            ins = o.fn(e)
            if o.dma:
                ins.then_inc(self.sem_d[o.slot], 16)
            elif o.signal:
                ins.then_inc(self.sem_c[o.stream], 1)
            o.fn = None
            o.deps = None
        self.n_ops += len(self.batch)
        self.batch = []
        self.last_w = {}
        self.readers = {}
        self.bar_deps = [o for o in self.last_op.values() if o is not None] + [st[0] for st in self.slots.values()]
        self.bar_pending = {s: True for s in self.STREAMS}

    def finish(self):
        self.flush()
        for key, st in self.slots.items():
            self.nc.sync.wait_ge(self.sem_d[key], st[1])


def _tiles(a, b, tmax):
    n = -(-(b - a) // tmax)
    base, rem = divmod(b - a, n)
    out = []
    t = a
    for i in range(n):
        sz = base + (1 if i < rem else 0)
        out.append((t, t + sz))
        t += sz
    return out


def _nblocks(W):
    return [(n0, min(n0 + 512, W)) for n0 in range(0, W, 512)]


class Builder:
    def __init__(self, n_layers=4, dbg=None):
        self.n_layers = n_layers
        self.dbg = dbg
        nc = self.nc = bass.Bass("TRN2", target_bir_lowering=False)
        self.S = Sched(nc)
        dt = nc.dram_tensor

        def inp(name, shape, dtype=F32):
            return dt(name, list(shape), dtype, kind="ExternalInput").ap()

        def scr(name, shape, dtype=F32):
            return dt(name, list(shape), dtype, kind="Internal").ap()

        self.xin = inp("xin", [D, TE])
        self.cin = inp("cin", [D, TEC])
        self.vmask = inp("vmask", [128, TE])
        self.rcnt = inp("rcnt", [128, 4, TE])
        self.vmask_c = inp("vmask_c", [128, TEC])
        self.rcnt_c = inp("rcnt_c", [128, 4, TEC])
        self.cs = inp("cs", [128, KC, 2])
        self.bmod = inp("bmod", [4, 128, 96])
        self.w_mod = inp("w_mod", [4, D, 6 * D])
        self.pool_w = inp("pool_w", [2, 4, 512, 512])
        self.pool_ls = inp("pool_ls", [2, 128, KC])
        self.w_qkv = inp("w_qkv", [D, 3 * D])
        self.w_o = inp("w_o", [D, D])
        self.btab = inp("btab", [NHEAD, NTAB, 128, 128])
        self.w_pw1 = inp("w_pw1", [D, 2 * D])
        self.w_pw2 = inp("w_pw2", [D, D])
        self.cvv = inp("cvv", [128, KC, 40])
        self.w_up = inp("w_up", [4, D, 2 * DFF])
        self.w_dn = inp("w_dn", [4, DFF, D])
        self.cw = inp("cw", [128, 4, 2 * FC, 4])
        self.fng = inp("fng", [128, KC])
        self.ident_in = inp("ident_in", [128, 128])
        self.outT = dt("outT", [D, OWN], F32, kind="ExternalOutput").ap()
        self.xa = scr("xa", [D, TE])
        self.xb = scr("xb", [D, TE])
        self.ca = scr("ca", [D, TEC])
        self.cb = scr("cb", [D, TEC])
        self.QT = scr("QT", [D, TE], BF16)
        self.KT = scr("KT", [D, TE], BF16)
        self.VA = scr("VA", [NPAIR, TE, 130], BF16)
        self.KcT = scr("KcT", [D, CTXL], BF16)
        self.VcA = scr("VcA", [NPAIR, CTXL, 130], BF16)
        self.OT = scr("OT", [D, TE], BF16)
        self.dbg_outs = {}
        for name in (dbg.split(",") if dbg else []):
            t = getattr(self, name)
            self.dbg_outs[name] = dt("dbg_" + name, list(t.shape), t.dtype, kind="ExternalOutput").ap()

    def sb(self, st, name, shape, dtype):
        self.uid = getattr(self, "uid", 0) + 1
        return st.enter_context(self.nc.sbuf_tensor("%s_%d" % (name, self.uid), list(shape), dtype))

    def ps(self, st, name, shape, dtype=F32):
        self.uid = getattr(self, "uid", 0) + 1
        return st.enter_context(self.nc.psum_tensor("%s_%d" % (name, self.uid), list(shape), dtype))

    def consts(self, st):
        S = self.S
        self.ones = self.sb(st, "ones", [128, 128], BF16)
        self.ident = self.sb(st, "ident", [128, 128], BF16)
        self.M = [self.sb(st, "M%d" % l, [128, 2, 96], F32) for l in range(4)]
        self.cws = self.sb(st, "cws", [128, 4, 2 * FC, 4], F32)
        self.cvs = self.sb(st, "cvs", [128, KC, 40], F32)
        self.fngs = self.sb(st, "fngs", [128, KC], F32)
        self.pls = self.sb(st, "pls", [128, 2, KC], F32)
        self.gls = self.sb(st, "gls", [128, KC], F32)
        self.cbf = self.sb(st, "cbf", [128, KC, 2], BF16)
        self.bms = self.sb(st, "bms", [128, 4, 96], F32)
        S.op("dve", lambda e: e.memset(self.ones[:, :], 1.0), w=["ones"])
        S.dma("sp", self.cws[:, :, :, :], self.cw, w=["cws"])
        S.dma("sp", self.cvs[:, :, :], self.cvv, w=["cvs"])
        S.dma("sp", self.fngs[:, :], self.fng, w=["fngs"])
        S.dma("pool", self.ident[:, :], self.ident_in, w=["ident"])
        for j in range(2):
            S.dma("sp", self.pls[:, j, :], self.pool_ls[j], w=["pls"])

    def mod(self, l, who, k, c):
        return self.M[l][:, who, k * KC + c:k * KC + c + 1]

    def mod_unit(self, l, cg, wb, pm, slot):
        S = self.S
        cb = self.cbf
        S.dma("pool", wb[slot][:, :, :],
              self.w_mod[l][:, cg * 512:(cg + 1) * 512].rearrange("(k p) c -> p k c", p=128), w=[("wm", slot)])
        for f4 in range(4):
            fo = cg * 4 + f4
            for k in range(KC):
                S.op("pe", (lambda e, f4=f4, fo=fo, k=k: e.matmul(
                    pm[:, fo * 2:fo * 2 + 2], lhsT=wb[slot][:, k, f4 * 128:(f4 + 1) * 128], rhs=cb[:, k, :],
                    start=(k == 0), stop=(k == KC - 1))),
                    r=[("wm", slot), "cb"], w=[("pm", l % 2)])

    def mod_fin(self, l, pm):
        S = self.S
        for who in range(2):
            S.op("dve", (lambda e, who=who: e.tensor_tensor(
                out=self.M[l][:, who, :], in0=pm[:, 0:192].rearrange("p (f w) -> p f w", w=2)[:, :, who],
                in1=self.bms[:, l, :], op=ALU.add)), r=[("pm", l % 2), "bm"], w=[("M", l)])
            for k in (1, 4):
                S.op("dve", (lambda e, who=who, k=k: e.tensor_scalar_add(
                    out=self.M[l][:, who, k * KC:(k + 1) * KC], in0=self.M[l][:, who, k * KC:(k + 1) * KC],
                    scalar1=1.0)), r=[("M", l)], w=[("M", l)])

    def phase_mod(self):
        S, nc = self.S, self.nc
        with ExitStack() as st:
            c32 = self.sb(st, "c32", [128, KC, 2], F32)
            sg = self.sb(st, "csg", [128, KC, 2], F32)
            wb = [self.sb(st, "wm%d" % i, [128, KC, 512], BF16) for i in range(3)]
            pm = [self.ps(st, "pm%d" % i, [128, 512]) for i in range(2)]
            S.dma("sp", c32[:, :, :], self.cs, w=["c32"])
            for l in range(4):
                S.dma("sp", self.bms[:, l, :], self.bmod[l], w=["bm"])
            S.op("act", lambda e: e.activation(out=sg[:, :, :], in_=c32[:, :, :], func=AF.Sigmoid), r=["c32"], w=["sg"])
            S.op("dve", lambda e: e.tensor_tensor(out=self.cbf[:, :, :], in0=c32[:, :, :], in1=sg[:, :, :], op=ALU.mult),
                 r=["c32", "sg"], w=["cb"])
            if self.n_layers >= 1:
                for cg in range(24):
                    self.mod_unit(0, cg, wb, pm[0], cg % 3)
                self.mod_fin(0, pm[0])
            S.flush()

    def norm_stats(self, src, c0, c1, bufs, ps, pkey, vm_dram=None):
        S = self.S
        W = c1 - c0
        xg, sq, rstd = bufs["xg"], bufs["sq"], bufs["rstd"]
        gs = bufs["gs"]
        ng = KC // gs
        for g in range(ng):
            s = g % 2
            S.dma("sp", xg[s][:, :, :W],
                  src[g * gs * 128:(g + 1) * gs * 128, c0:c1].rearrange("(k p) t -> p k t", p=128), w=[("xg", s)])
            S.op("act", (lambda e, s=s: e.activation(out=sq[s][:, :, :W], in_=xg[s][:, :, :W], func=AF.Square)),
                 r=[("xg", s)], w=[("sq", s)])
            for kk in range(gs):
                for (n0, n1) in _nblocks(W):
                    S.op("pe", (lambda e, s=s, kk=kk, n0=n0, n1=n1, g=g: e.matmul(
                        ps[:, n0:n1], lhsT=self.ones[:, :], rhs=sq[s][:, kk, n0:n1],
                        start=(g == 0 and kk == 0), stop=(g == ng - 1 and kk == gs - 1))),
                        r=[("sq", s), "ones"], w=[pkey])
        S.op("dve", (lambda e: e.tensor_scalar(out=rstd[:, :W], in0=ps[:, :W], scalar1=1.0 / D, scalar2=EPS,
                                               op0=ALU.mult, op1=ALU.add)), r=[pkey], w=["rstd"])
        S.op("act", (lambda e: e.activation(out=rstd[:, :W], in_=rstd[:, :W], func=AF.Sqrt)), r=["rstd"], w=["rstd"])
        S.op("dve", (lambda e: e.reciprocal(out=rstd[:, :W], in_=rstd[:, :W])), r=["rstd"], w=["rstd"])
        if vm_dram is not None:
            vm = bufs["vm"]
            S.dma("sp", vm[:, :W], vm_dram[:, c0:c1], w=["vm"])
            S.op("dve", (lambda e: e.tensor_tensor(out=rstd[:, :W], in0=rstd[:, :W], in1=vm[:, :W], op=ALU.mult)),
                 r=["rstd", "vm"], w=["rstd"])

    def ada_h(self, src, c0, c1, bufs, l, who, kshift, hout, masked):
        S = self.S
        W = c1 - c0
        xg, rstd = bufs["xg"], bufs["rstd"]
        gs = bufs["gs"]
        for g in range(KC // gs):
            s = g % 2
            S.dma("sp", xg[s][:, :, :W],
                  src[g * gs * 128:(g + 1) * gs * 128, c0:c1].rearrange("(k p) t -> p k t", p=128), w=[("xg", s)])
            for kk in range(gs):
                k = gs * g + kk
                S.op("dve", (lambda e, s=s, kk=kk: e.tensor_tensor(
                    out=xg[s][:, kk, :W], in0=xg[s][:, kk, :W], in1=rstd[:, :W], op=ALU.mult)),
                    r=[("xg", s), "rstd"], w=[("xg", s)])
                S.op("act", (lambda e, s=s, kk=kk, k=k: e.activation(
                    out=hout[:, k, :W], in_=xg[s][:, kk, :W], func=AF.Identity,
                    bias=self.mod(l, who, kshift, k), scale=self.mod(l, who, kshift + 1, k))),
                    r=[("xg", s), ("M", l)], w=[("h", k)])
                if masked:
                    S.op("dve", (lambda e, k=k: e.tensor_tensor(
                        out=hout[:, k, :W], in0=hout[:, k, :W], in1=bufs["vm"][:, :W], op=ALU.mult)),
                        r=[("h", k), "vm"], w=[("h", k)])

    def norm_bufs(self, st, wmax, with_vm=True, gs=2):
        b = {"xg": [self.sb(st, "xg%d" % i, [128, gs, wmax], F32) for i in range(2)],
             "sq": [self.sb(st, "sq%d" % i, [128, gs, wmax], BF16) for i in range(2)],
             "rstd": self.sb(st, "rstd", [128, wmax], F32), "gs": gs}
        if with_vm:
            b["vm"] = self.sb(st, "vm", [128, wmax], F32)
        return b

    def mm_fm(self, ps, pkey, wslot, wkey, nk, rhs, rkeyf, W, wcol0=0):
        S = self.S
        for (n0, n1) in _nblocks(W):
            for k in range(nk):
                S.op("pe", (lambda e, k=k, n0=n0, n1=n1: e.matmul(
                    ps[:, n0:n1], lhsT=wslot[:, k, wcol0:wcol0 + 128], rhs=rhs[:, k, n0:n1],
                    start=(k == 0), stop=(k == nk - 1))),
                    r=[wkey, rkeyf(k)], w=[pkey])

    def phase_pool(self, l, j, who, src, dst, a, b, tmax, inval, vm_dram, rc_dram, bg=()):
        S, nc = self.S, self.nc
        with ExitStack() as st:
            wmax = tmax + 16
            nb = self.norm_bufs(st, wmax)
            pw = self.sb(st, "pw", [128, 4, 4, 512], BF16)
            dd = self.sb(st, "dd", [128, KC, tmax], BF16)
            hn = [self.sb(st, "hn%d" % i, [128, wmax], F32) for i in range(2)]
            sa = [self.sb(st, "sa%d" % i, [128, wmax], F32) for i in range(2)]
            sb2 = [self.sb(st, "sbb%d" % i, [128, wmax], F32) for i in range(2)]
            rc = self.sb(st, "rc", [128, 4, tmax], F32)
            xe = [self.sb(st, "xe%d" % i, [128, tmax], F32) for i in range(2)]
            xo = [self.sb(st, "xo%d" % i, [128, tmax], F32) for i in range(2)]
            nps = 3 if bg else 4
            ps = [self.ps(st, "pp%d" % i, [128, 1024]) for i in range(nps)]
            if bg:
                wmb = [self.sb(st, "wmb%d" % i, [128, KC, 512], BF16) for i in range(3)]
                pmb = [self.ps(st, "pmb%d" % i, [128, 512]) for i in range(2)]
                bg_units = [(bl, cg) for bl in bg for cg in range(24)]
            else:
                bg_units = []
            bg_state = {"pos": 0}
            for g in range(4):
                S.dma("pool", pw[:, g, :, :], self.pool_w[j][g].rearrange("(c p) e -> p c e", p=128), w=["pw"])
            S.op("dve", lambda e: e.tensor_tensor(out=self.gls[:, :], in0=self.M[l][:, who, 2 * KC:3 * KC],
                                                  in1=self.pls[:, j, :], op=ALU.mult),
                 r=[("M", l), "pls"], w=["gls"])
            def body(t0, t1):
                Tt = t1 - t0
                c0, c1 = t0 - 8, t1 + 8
                W = c1 - c0
                masked = c0 < inval[0] or c1 > inval[1]
                self.norm_stats(src, c0, c1, nb, ps[0], ("ps", 0), vm_dram if masked else None)
                S.dma("sp", rc[:, :, :Tt], rc_dram[:, :, t0:t1], w=["rc"])
                xg = nb["xg"]
                for g in range(8):
                    s = g % 2
                    S.dma("sp", xg[s][:, :, :W], src[g * 256:(g + 1) * 256, c0:c1].rearrange("(k p) t -> p k t", p=128),
                          w=[("xg", s)])
                    for kk in range(2):
                        k = 2 * g + kk
                        grp = k // 4
                        en = "dve"
                        h_, a_, b_ = hn[kk], sa[kk], sb2[kk]
                        hk, ak, bk = ("hn", kk), ("sa", kk), ("sb", kk)
                        S.op(en, (lambda e, s=s, kk=kk, h_=h_: e.tensor_tensor(
                            out=h_[:, :W], in0=xg[s][:, kk, :W], in1=nb["rstd"][:, :W], op=ALU.mult)),
                            r=[("xg", s), "rstd"], w=[hk])
                        S.op(en, (lambda e, h_=h_, a_=a_: e.tensor_tensor(
                            out=a_[:, 1:W], in0=h_[:, 0:W - 1], in1=h_[:, 1:W], op=ALU.add)), r=[hk], w=[ak])
                        cur, ck, oth, ok = a_, ak, b_, bk
                        lo, hi = 1, W
                        sh = 1
                        for step in range(grp):
                            nlo, nhi = lo + sh, hi - sh
                            S.op(en, (lambda e, cur=cur, oth=oth, nlo=nlo, nhi=nhi, sh=sh: e.tensor_tensor(
                                out=oth[:, nlo:nhi], in0=cur[:, nlo - sh:nhi - sh], in1=cur[:, nlo + sh:nhi + sh],
                                op=ALU.add)), r=[ck], w=[ok])
                            cur, ck, oth, ok = oth, ok, cur, ck
                            lo, hi = nlo, nhi
                            sh *= 2
                        S.op(en, (lambda e, cur=cur, oth=oth, grp=grp: e.tensor_tensor(
                            out=oth[:, 8:8 + Tt], in0=cur[:, 8:8 + Tt], in1=rc[:, grp, :Tt], op=ALU.mult)),
                            r=[ck, "rc"], w=[ok])
                        S.op(en, (lambda e, oth=oth, h_=h_: e.tensor_tensor(
                            out=oth[:, 8:8 + Tt], in0=oth[:, 8:8 + Tt], in1=h_[:, 8:8 + Tt], op=ALU.subtract)),
                            r=[ok, hk], w=[ok])
                        S.op("act", (lambda e, oth=oth, k=k: e.activation(
                            out=dd[:, k, :Tt], in_=oth[:, 8:8 + Tt], func=AF.Copy, scale=self.mod(l, who, 1, k))),
                            r=[ok, ("M", l)], w=[("dd", k)])
                done_layers = []
                if bg_units:
                    ntl = len(_tiles(a, b, tmax))
                    per = -(-len(bg_units) // ntl)
                    for _ in range(per):
                        if bg_state["pos"] >= len(bg_units):
                            break
                        bl, cg = bg_units[bg_state["pos"]]
                        self.mod_unit(bl, cg, wmb, pmb[bl % 2], bg_state["pos"] % 3)
                        bg_state["pos"] += 1
                        if cg == 23:
                            done_layers.append(bl)
                for ec in range(KC):
                    grp = ec // 4
                    s = ec % 2
                    p = ps[1 + ec % (nps - 1)]
                    pk = ("ps", 1 + ec % (nps - 1))
                    for (n0, n1) in _nblocks(Tt):
                        for cc in range(4):
                            S.op("pe", (lambda e, p=p, grp=grp, cc=cc, ec=ec, n0=n0, n1=n1: e.matmul(
                                p[:, n0:n1], lhsT=pw[:, grp, cc, (ec % 4) * 128:(ec % 4 + 1) * 128],
                                rhs=dd[:, grp * 4 + cc, n0:n1], start=(cc == 0), stop=(cc == 3))),
                                r=["pw", ("dd", grp * 4 + cc)], w=[pk])
                    S.dma("sp", xe[s][:, :Tt], src[ec * 128:(ec + 1) * 128, t0:t1], w=[("xe", s)])
                    S.op("dve", (lambda e, p=p, s=s, ec=ec: e.scalar_tensor_tensor(
                        out=xo[s][:, :Tt], in0=p[:, :Tt], scalar=self.gls[:, ec:ec + 1], in1=xe[s][:, :Tt],
                        op0=ALU.mult, op1=ALU.add)), r=[pk, "gls", ("xe", s)], w=[("xo", s)])
                    S.dma("sp", dst[ec * 128:(ec + 1) * 128, t0:t1], xo[s][:, :Tt], r=[("xo", s)], w=[("dst", ec)])
                for bl in done_layers:
                    self.mod_fin(bl, pmb[bl % 2])
            for (t0, t1) in _tiles(a, b, tmax):
                body(t0, t1)
            S.flush()

    def phase_ffn(self, l, who, src, dst, a, b, tmax, inval, vm_dram):
        S, nc = self.S, self.nc
        with ExitStack() as st:
            wmax = tmax + 2
            nb = self.norm_bufs(st, wmax, gs=1)
            h2 = self.sb(st, "h2", [128, KC, wmax], BF16)
            aa = self.sb(st, "aa", [128, FC, tmax], BF16)
            wup = [self.sb(st, "wup%d" % i, [128, KC, 256], BF16) for i in range(2)]
            wdn = [self.sb(st, "wdn%d" % i, [128, FC, 128], BF16) for i in range(2)]
            tg = [self.sb(st, "tg%d" % i, [128, tmax], F32) for i in range(2)]
            tv = [self.sb(st, "tv%d" % i, [128, tmax], F32) for i in range(2)]
            ps = [self.ps(st, "pf%d" % i, [128, 1024]) for i in range(4)]
            xe, xo = tg, tv
            tl = _tiles(a, b, tmax)

            def prep(i, stage):
                t0, t1 = tl[i]
                c0, c1 = t0 - 1, t1 + 1
                masked = c0 < inval[0] or c1 > inval[1]
                if stage == 0:
                    self.norm_stats(src, c0, c1, nb, ps[3], ("ps", 3), vm_dram if masked else None)
                else:
                    self.ada_h(src, c0, c1, nb, l, who, 3, h2, masked)

            def body(i):
                t0, t1 = tl[i]
                Tt = t1 - t0
                W = Tt + 2
                for j in range(FC):
                    s = j % 2
                    for half, col in enumerate((j * 128, DFF + j * 128)):
                        S.dma("pool", wup[s][:, :, half * 128:(half + 1) * 128],
                              self.w_up[l][:, col:col + 128].rearrange("(k p) c -> p k c", p=128),
                              w=[("wup", s, half)])
                    for half in range(2):
                        pi = (2 * j + half) % 4
                        self.mm_fm(ps[pi], ("ps", pi), wup[s], ("wup", s, half), KC, h2, lambda k: ("h", k), W,
                                   wcol0=half * 128)
                    for half, tt, tk in ((0, tg[s], ("tg", s)), (1, tv[s], ("tv", s))):
                        pi = (2 * j + half) % 4
                        pt = ps[pi]
                        ci = half * FC + j
                        S.op("act", (lambda e, pt=pt, tt=tt, ci=ci: e.activation(
                            out=tt[:, :Tt], in_=pt[:, 1:Tt + 1], func=AF.Identity,
                            bias=self.cws[:, l, ci, 3:4], scale=self.cws[:, l, ci, 1:2])),
                            r=[("ps", pi), "cws"], w=[tk])
                        S.op("dve", (lambda e, pt=pt, tt=tt, ci=ci: e.scalar_tensor_tensor(
                            out=tt[:, :Tt], in0=pt[:, 0:Tt], scalar=self.cws[:, l, ci, 0:1], in1=tt[:, :Tt],
                            op0=ALU.mult, op1=ALU.add)), r=[("ps", pi), "cws", tk], w=[tk])
                        S.op("dve", (lambda e, pt=pt, tt=tt, ci=ci: e.scalar_tensor_tensor(
                            out=tt[:, :Tt], in0=pt[:, 2:Tt + 2], scalar=self.cws[:, l, ci, 2:3], in1=tt[:, :Tt],
                            op0=ALU.mult, op1=ALU.add)), r=[("ps", pi), "cws", tk], w=[tk])
                    S.op("act", (lambda e, s=s: e.activation(out=tg[s][:, :Tt], in_=tg[s][:, :Tt], func=AF.Silu)),
                         r=[("tg", s)], w=[("tg", s)])
                    S.op("dve", (lambda e, s=s, j=j: e.tensor_tensor(
                        out=aa[:, j, :Tt], in0=tg[s][:, :Tt], in1=tv[s][:, :Tt], op=ALU.mult)),
                        r=[("tg", s), ("tv", s)], w=[("aa", j)])
                for ec in range(KC):
                    s = ec % 2
                    pi = ec % 4
                    S.dma("pool", wdn[s][:, :, :],
                          self.w_dn[l][:, ec * 128:(ec + 1) * 128].rearrange("(j p) c -> p j c", p=128),
                          w=[("wdn", s)])
                    self.mm_fm(ps[pi], ("ps", pi), wdn[s], ("wdn", s), FC, aa, lambda k: ("aa", k), Tt)
                    S.dma("sp", xe[s][:, :Tt], src[ec * 128:(ec + 1) * 128, t0:t1], w=[("tg", s)])
                    S.op("dve", (lambda e, s=s, pi=pi, ec=ec: e.scalar_tensor_tensor(
                        out=xo[s][:, :Tt], in0=ps[pi][:, :Tt], scalar=self.mod(l, who, 5, ec), in1=xe[s][:, :Tt],
                        op0=ALU.mult, op1=ALU.add)), r=[("ps", pi), ("M", l), ("tg", s)], w=[("tv", s)])
                    S.dma("sp", dst[ec * 128:(ec + 1) * 128, t0:t1], xo[s][:, :Tt], r=[("tv", s)], w=[("dst", ec)])
                    if i + 1 < len(tl):
                        if ec == 2:
                            prep(i + 1, 0)
                        if ec == 6:
                            prep(i + 1, 1)
            prep(0, 0)
            prep(0, 1)
            for i in range(len(tl)):
                body(i)
            S.flush()

    def phase_qkv(self, who, src, tiles, ctx):
        S = self.S
        l = 1
        with ExitStack() as st:
            tmax = 1024
            nb = self.norm_bufs(st, tmax, with_vm=False, gs=2)
            h = self.sb(st, "hq", [128, KC, tmax], BF16)
            wq = [self.sb(st, "wq%d" % i, [128, KC, 256], BF16) for i in range(2)]
            wv = [self.sb(st, "wv%d" % i, [128, KC, 512], BF16) for i in range(2)]
            qk = [self.sb(st, "qk%d" % i, [128, tmax], BF16) for i in range(2)]
            vst = [self.sb(st, "vst%d" % i, [128, 8, 65], BF16) for i in range(2)]
            ps = [self.ps(st, "pq%d" % i, [128, 1024]) for i in range(4)]
            for i in range(2):
                S.op("dve", (lambda e, i=i: e.memset(vst[i][:, :, :], 1.0)), w=[("vst", i)])

            def body(t0, t1):
                Tt = t1 - t0
                self.norm_stats(src, t0, t1, nb, ps[0], ("ps", 0))
                self.ada_h(src, t0, t1, nb, l, who, 0, h, False)
                fo0 = 16 if ctx else 0
                for pidx, fo2 in enumerate(range(fo0, 32, 2)):
                    s = pidx % 2
                    S.dma("pool", wq[s][:, :, :],
                          self.w_qkv[:, fo2 * 128:fo2 * 128 + 256].rearrange("(k p) c -> p k c", p=128), w=[("wq", s)])
                    for half in range(2):
                        fo = fo2 + half
                        pi = (2 * pidx + half) % 4
                        self.mm_fm(ps[pi], ("ps", pi), wq[s], ("wq", s), KC, h, lambda k: ("h", k), Tt, wcol0=half * 128)
                        qs = qk[fo % 2]
                        sc = 0.125 if fo < 16 else 1.0
                        S.op("act", (lambda e, qs=qs, pi=pi, sc=sc: e.activation(
                            out=qs[:, :Tt], in_=ps[pi][:, :Tt], func=AF.Copy, scale=sc)),
                            r=[("ps", pi)], w=[("qk", fo % 2)])
                        if fo < 16:
                            dst = self.QT[fo * 128:(fo + 1) * 128, t0:t1]
                        elif ctx:
                            dst = self.KcT[(fo - 16) * 128:(fo - 15) * 128, t0 - CPAD:t1 - CPAD]
                        else:
                            dst = self.KT[(fo - 16) * 128:(fo - 15) * 128, t0:t1]
                        S.dma("sp", dst, qs[:, :Tt], r=[("qk", fo % 2)], w=[("qkd", fo)])
                ntb = Tt // 128
                for cg in range(4):
                    s = cg % 2
                    S.dma("pool", wv[s][:, :, :],
                          self.w_qkv[:, 2 * D + cg * 512:2 * D + (cg + 1) * 512].rearrange("(k p) c -> p k c", p=128),
                          w=[("wv", s)])
                    for tb in range(ntb):
                        u = cg * ntb + tb
                        pi = u % 4
                        for k in range(KC):
                            S.op("pe", (lambda e, pi=pi, k=k, tb=tb, s=s: e.matmul(
                                ps[pi][:, 0:512], lhsT=h[:, k, tb * 128:(tb + 1) * 128], rhs=wv[s][:, k, :],
                                start=(k == 0), stop=(k == KC - 1))), r=[("h", k), ("wv", s)], w=[("ps", pi)])
                        vs = vst[u % 2]
                        S.op("dve", (lambda e, vs=vs, pi=pi: e.tensor_copy(
                            out=vs[:, :, 0:64], in_=ps[pi][:, 0:512].rearrange("p (h d) -> p h d", d=64))),
                            r=[("ps", pi)], w=[("vst", u % 2)])
                        if ctx:
                            dst = self.VcA[4 * cg:4 * cg + 4, t0 - CPAD + tb * 128:t0 - CPAD + (tb + 1) * 128, :]
                        else:
                            dst = self.VA[4 * cg:4 * cg + 4, t0 + tb * 128:t0 + (tb + 1) * 128, :]
                        S.dma("sp", dst.rearrange("a t c -> t a c"),
                              vs[:, :, :].rearrange("p (a b) c -> p a (b c)", b=2), r=[("vst", u % 2)], w=[("vd", u)])
            for (t0, t1) in tiles:
                body(t0, t1)
            S.flush()

    def phase_att(self):
        S, nc = self.S, self.nc
        NBLK = TE // 128
        with ExitStack() as st:
            ksb = [self.sb(st, "ksb%d" % i, [128, TE], BF16) for i in range(2)]
            qsb = [self.sb(st, "qsb%d" % i, [128, TE], BF16) for i in range(2)]
            vsb = [self.sb(st, "vsb%d" % i, [128, NBLK, 130], BF16) for i in range(2)]
            kcs = [self.sb(st, "kcs%d" % i, [128, CTXL], BF16) for i in range(2)]
            vcs = [self.sb(st, "vcs%d" % i, [128, 2, 130], BF16) for i in range(2)]
            bt = [self.sb(st, "bt%d" % i, [128, 2, NTAB, 128], F32) for i in range(2)]
            ssb = [self.sb(st, "ssb%d" % i, [128, 6, 128], F32) for i in range(2)]
            pT = [self.sb(st, "pT%d" % i, [128, 1024], BF16) for i in range(2)]
            osb = [self.sb(st, "osb%d" % i, [128, 128], BF16) for i in range(2)]
            rden = [self.sb(st, "rden%d" % i, [128, 2], F32) for i in range(2)]
            otsb = [self.sb(st, "otsb%d" % i, [128, NQB * 128], BF16) for i in range(2)]
            ps_s = [self.ps(st, "pss%d" % i, [128, 1024]) for i in range(2)]
            ps_o = [self.ps(st, "pso%d" % i, [128, 512]) for i in range(2)]
            ps_t = [self.ps(st, "pst%d" % i, [128, 128], BF16) for i in range(2)]
            def loads(pr):
                s = pr % 2
                rows = slice(pr * 128, (pr + 1) * 128)
                S.dma("sp", ksb[s][:, :], self.KT[rows, :], w=[("k", s)])
                S.dma("sp", qsb[s][:, :], self.QT[rows, :], w=[("q", s)])
                S.dma("sp", vsb[s][:, :, :], self.VA[pr].rearrange("(b p) c -> p b c", p=128), w=[("v", s)])
                S.dma("sp", kcs[s][:, :], self.KcT[rows, :], w=[("kc", s)])
                S.dma("sp", vcs[s][:, :, :], self.VcA[pr].rearrange("(b p) c -> p b c", p=128), w=[("vc", s)])
                for hh in range(2):
                    S.dma("sp", bt[s][:, hh, :, :], self.btab[2 * pr + hh].rearrange("j k q -> k j q"), w=[("bt", s, hh)])

            def unit_info(n):
                pr, rem = divmod(n, NQB * 2)
                qb, hh = divmod(rem, 2)
                qbi = QB0 + qb
                if qbi == 3:
                    cls, kbs = "A", list(range(qbi - 2, qbi + 4))
                elif qbi == 4:
                    cls, kbs = "B", list(range(qbi - 2, qbi + 3))
                elif qbi == 33:
                    cls, kbs = "C", list(range(qbi - 2, qbi + 3))
                elif qbi == 34:
                    cls, kbs = "D", list(range(qbi - 3, qbi + 3))
                else:
                    cls, kbs = "I", list(range(qbi - 2, qbi + 3))
                return pr, qb, hh, qbi, TAB_OFF[cls], kbs

            def front(n):
                pr, qb, hh, qbi, off, kbs = unit_info(n)
                s = pr % 2
                u = n % 2
                nbk = len(kbs)
                hp = slice(hh * 64, (hh + 1) * 64)
                pss = ps_s[u]
                for i, kb in enumerate(kbs):
                    S.op("pe", (lambda e, i=i, kb=kb: e.matmul(
                        pss[:, i * 128:(i + 1) * 128], lhsT=ksb[s][hp, kb * 128:(kb + 1) * 128],
                        rhs=qsb[s][hp, qbi * 128:(qbi + 1) * 128], start=True, stop=True)),
                        r=[("k", s), ("q", s)], w=[("pss", u)])
                for cbk in range(2):
                    S.op("pe", (lambda e, cbk=cbk: e.matmul(
                        pss[:, (nbk + cbk) * 128:(nbk + cbk + 1) * 128], lhsT=kcs[s][hp, cbk * 128:(cbk + 1) * 128],
                        rhs=qsb[s][hp, qbi * 128:(qbi + 1) * 128], start=True, stop=True)),
                        r=[("kc", s), ("q", s)], w=[("pss", u)])
                S.op("dve", (lambda e: e.tensor_tensor(
                    out=ssb[u][:, 0:nbk, :], in0=pss[:, 0:nbk * 128].rearrange("p (j q) -> p j q", q=128),
                    in1=bt[s][:, hh, off:off + nbk, :], op=ALU.add)),
                    r=[("pss", u), ("bt", s, hh)], w=[("ssb", u)])
                S.op("act", (lambda e: e.activation(
                    out=pT[u][:, 0:nbk * 128].rearrange("p (j q) -> p j q", q=128), in_=ssb[u][:, 0:nbk, :],
                    func=AF.Exp)), r=[("ssb", u)], w=[("pT", u)])
                S.op("act", (lambda e: e.activation(
                    out=pT[u][:, nbk * 128:(nbk + 2) * 128], in_=pss[:, nbk * 128:(nbk + 2) * 128],
                    func=AF.Exp)), r=[("pss", u)], w=[("pT", u)])

            def back(n):
                pr, qb, hh, qbi, off, kbs = unit_info(n)
                s = pr % 2
                u = n % 2
                nbk = len(kbs)
                hp = slice(hh * 64, (hh + 1) * 64)
                ob = qb % 2
                pso = ps_o[u]
                for i, kb in enumerate(kbs):
                    S.op("pe", (lambda e, i=i, kb=kb: e.matmul(
                        pso[:, 0:65], lhsT=pT[u][:, i * 128:(i + 1) * 128], rhs=vsb[s][:, kb, hh * 65:(hh + 1) * 65],
                        start=(i == 0), stop=False)), r=[("pT", u), ("v", s)], w=[("pso", u)])
                for cbk in range(2):
                    S.op("pe", (lambda e, cbk=cbk: e.matmul(
                        pso[:, 0:65], lhsT=pT[u][:, (nbk + cbk) * 128:(nbk + cbk + 1) * 128],
                        rhs=vcs[s][:, cbk, hh * 65:(hh + 1) * 65], start=False, stop=(cbk == 1))),
                        r=[("pT", u), ("vc", s)], w=[("pso", u)])
                S.op("dve", (lambda e: e.reciprocal(out=rden[u][:, 0:1], in_=pso[:, 64:65])),
                     r=[("pso", u)], w=[("rden", u)])
                S.op("act", (lambda e: e.activation(
                    out=osb[ob][:, hp], in_=pso[:, 0:64], func=AF.Copy, scale=rden[u][:, 0:1])),
                    r=[("pso", u), ("rden", u)], w=[("osb", ob)])
                if hh == 1:
                    S.op("pe", (lambda e: e.transpose(out=ps_t[ob][:, :], in_=osb[ob][:, :], identity=self.ident[:, :])),
                         r=[("osb", ob), "ident"], w=[("pst", ob)])
                    S.op("dve", (lambda e: e.tensor_copy(out=otsb[s][:, qb * 128:(qb + 1) * 128], in_=ps_t[ob][:, :])),
                         r=[("pst", ob)], w=[("otsb", s)])
                    if qb == NQB - 1:
                        rows = slice(pr * 128, (pr + 1) * 128)
                        S.dma("sp", self.OT[rows, QB0 * 128:(QB0 + NQB) * 128], otsb[s][:, :], r=[("otsb", s)],
                              w=[("otd", pr)])

            NU = NPAIR * NQB * 2
            loads(0)
            front(0)
            for n in range(NU):
                if n % (NQB * 2) == 0 and n // (NQB * 2) + 1 < NPAIR:
                    loads(n // (NQB * 2) + 1)
                if n + 1 < NU:
                    front(n + 1)
                back(n)
            S.flush()

    def phase_wo(self, l, a, b):
        S = self.S
        with ExitStack() as st:
            tmax = 1024
            ob = self.sb(st, "ob", [128, KC, tmax], BF16)
            wo = [self.sb(st, "wo%d" % i, [128, KC, 256], BF16) for i in range(2)]
            xe = [self.sb(st, "xe%d" % i, [128, tmax], F32) for i in range(2)]
            xo = [self.sb(st, "xo%d" % i, [128, tmax], F32) for i in range(2)]
            ps = [self.ps(st, "pw%d" % i, [128, 1024]) for i in range(4)]

            def body(t0, t1):
                Tt = t1 - t0
                S.dma("sp", ob[:, :, :Tt], self.OT[:, t0:t1].rearrange("(k p) t -> p k t", p=128), w=["ob"])
                for e2 in range(8):
                    s = e2 % 2
                    S.dma("pool", wo[s][:, :, :], self.w_o[:, e2 * 256:(e2 + 1) * 256].rearrange("(k p) c -> p k c", p=128),
                          w=[("wo", s)])
                    for half in range(2):
                        ec = 2 * e2 + half
                        pi = ec % 4
                        xs = ec % 2
                        self.mm_fm(ps[pi], ("ps", pi), wo[s], ("wo", s), KC, ob, lambda k: "ob", Tt, wcol0=half * 128)
                        S.dma("sp", xe[xs][:, :Tt], self.xa[ec * 128:(ec + 1) * 128, t0:t1], w=[("xe", xs)])
                        S.op("dve", (lambda e, pi=pi, xs=xs, ec=ec: e.scalar_tensor_tensor(
                            out=xo[xs][:, :Tt], in0=ps[pi][:, :Tt], scalar=self.mod(l, 0, 2, ec), in1=xe[xs][:, :Tt],
                            op0=ALU.mult, op1=ALU.add)), r=[("ps", pi), ("M", l), ("xe", xs)], w=[("xo", xs)])
                        S.dma("sp", self.xb[ec * 128:(ec + 1) * 128, t0:t1], xo[xs][:, :Tt], r=[("xo", xs)], w=[("dst", ec)])
            for (t0, t1) in _tiles(a, b, tmax):
                body(t0, t1)
            S.flush()

    def phase_convmod(self, l, src, dst, a, b):
        S = self.S
        with ExitStack() as st:
            tmax = 480
            wmax = tmax + 30
            nb = self.norm_bufs(st, wmax, with_vm=True, gs=2)
            h = self.sb(st, "hc", [128, KC, wmax], BF16)
            w1 = [self.sb(st, "wc1%d" % i, [128, KC, 256], BF16) for i in range(2)]
            w2 = [self.sb(st, "wc2%d" % i, [128, KC, 256], BF16) for i in range(2)]
            sig = [self.sb(st, "sig%d" % i, [128, wmax], F32) for i in range(2)]
            uu = [self.sb(st, "uu%d" % i, [128, wmax], F32) for i in range(2)]
            y = self.sb(st, "yc", [128, KC, tmax], F32)
            yb = [self.sb(st, "yb%d" % i, [128, tmax], BF16) for i in range(2)]
            ysq = [self.sb(st, "ysq%d" % i, [128, tmax], BF16) for i in range(2)]
            mean = self.sb(st, "mean", [128, tmax], F32)
            rs = self.sb(st, "rs", [128, tmax], F32)
            tn = [self.sb(st, "tn%d" % i, [128, tmax], F32) for i in range(2)]
            z = self.sb(st, "zc", [128, KC, tmax], BF16)
            xe = [self.sb(st, "cxe%d" % i, [128, tmax], F32) for i in range(2)]
            xo = [self.sb(st, "cxo%d" % i, [128, tmax], F32) for i in range(2)]
            gb2 = self.sb(st, "gb2", [128, KC], F32)
            NPE = 12
            TP0 = 31 - NPE
            dgs = self.sb(st, "dgs", [128, KC * NPE, 128], BF16)
            ub = [self.sb(st, "ub%d" % i, [128, wmax], BF16) for i in range(2)]
            ps = [self.ps(st, "pc%d" % i, [128, 512]) for i in range(8)]
            cvs = self.cvs
            for c in range(KC):
                for i in range(NPE):
                    S.op("dve", (lambda e, c=c, i=i: e.tensor_scalar_mul(
                        out=dgs[:, c * NPE + i, :], in0=self.ident[:, :], scalar1=cvs[:, c, TP0 + i:TP0 + i + 1])),
                        r=["ident", "cvs"], w=["dgs"])
            S.op("dve", lambda e: e.tensor_tensor(out=gb2[:, :], in0=self.M[l][:, 0, 2 * KC:3 * KC], in1=cvs[:, :, 36],
                                                  op=ALU.mult), r=[("M", l), "cvs"], w=["gb2"])

            def body(t0, t1):
                Tt = t1 - t0
                c0, c1 = t0 - 15, t1 + 15
                W = c1 - c0
                masked = c0 < O0 or c1 > E0
                self.norm_stats(src, c0, c1, nb, ps[0], ("ps", 0))
                self.ada_h(src, c0, c1, nb, l, 0, 0, h, False)
                if masked:
                    S.dma("sp", nb["vm"][:, :W], self.vmask[:, c0:c1], w=["vm"])
                def glu(c):
                    s = c % 2
                    for half, col in enumerate((c * 128, D + c * 128)):
                        S.dma("pool", w1[s][:, :, half * 128:(half + 1) * 128],
                              self.w_pw1[:, col:col + 128].rearrange("(k p) c -> p k c", p=128), w=[("w1", s, half)])
                    pa, pg = (2 * c) % 4, (2 * c + 1) % 4
                    self.mm_fm(ps[pa], ("ps", pa), w1[s], ("w1", s, 0), KC, h, lambda k: ("h", k), W, wcol0=0)
                    self.mm_fm(ps[pg], ("ps", pg), w1[s], ("w1", s, 1), KC, h, lambda k: ("h", k), W, wcol0=128)
                    S.op("act", (lambda e: e.activation(
                        out=sig[s][:, :W], in_=ps[pg][:, :W], func=AF.Sigmoid, bias=cvs[:, c, 35:36])),
                        r=[("ps", pg), "cvs"], w=[("sig", s)])

                def glu2(c):
                    s = c % 2
                    pa = (2 * c) % 4
                    S.op("dve", (lambda e: e.scalar_tensor_tensor(
                        out=uu[s][:, :W], in0=ps[pa][:, :W], scalar=cvs[:, c, 34:35], in1=sig[s][:, :W],
                        op0=ALU.add, op1=ALU.mult)), r=[("ps", pa), "cvs", ("sig", s)], w=[("uu", s)])
                    if masked:
                        S.op("dve", (lambda e: e.tensor_tensor(
                            out=uu[s][:, :W], in0=uu[s][:, :W], in1=nb["vm"][:, :W], op=ALU.mult)),
                            r=[("uu", s), "vm"], w=[("uu", s)])
                    S.op("act", (lambda e: e.copy(out=ub[s][:, :W], in_=uu[s][:, :W])), r=[("uu", s)], w=[("ub", s)])

                glu(0)
                glu2(0)
                for c in range(KC):
                    s = c % 2
                    pcv = 6 + c % 2
                    for i in range(NPE):
                        S.op("pe", (lambda e, s=s, c=c, i=i, pcv=pcv: e.matmul(
                            ps[pcv][:, :Tt], lhsT=dgs[:, c * NPE + i, :], rhs=ub[s][:, TP0 + i:TP0 + i + Tt],
                            start=(i == 0), stop=(i == NPE - 1))), r=["dgs", ("ub", s)], w=[("ps", pcv)])
                    if c + 1 < KC:
                        glu(c + 1)
                    en = "dve"
                    S.op(en, (lambda e, s=s, c=c: e.tensor_scalar(
                        out=y[:, c, :Tt], in0=uu[s][:, 0:Tt], scalar1=cvs[:, c, 0:1], scalar2=cvs[:, c, 31:32],
                        op0=ALU.mult, op1=ALU.add)), r=[("uu", s), "cvs"], w=[("y", c)])
                    for tp in range(1, TP0):
                        S.op(en, (lambda e, s=s, c=c, tp=tp: e.scalar_tensor_tensor(
                            out=y[:, c, :Tt], in0=uu[s][:, tp:tp + Tt], scalar=cvs[:, c, tp:tp + 1], in1=y[:, c, :Tt],
                            op0=ALU.mult, op1=ALU.add)), r=[("uu", s), "cvs", ("y", c)], w=[("y", c)])
                    S.op(en, (lambda e, c=c, pcv=pcv: e.tensor_tensor(
                        out=y[:, c, :Tt], in0=y[:, c, :Tt], in1=ps[pcv][:, :Tt], op=ALU.add)),
                        r=[("y", c), ("ps", pcv)], w=[("y", c)])
                    if c + 1 < KC:
                        glu2(c + 1)
                    S.op("act", (lambda e, s=s, c=c: e.copy(out=yb[s][:, :Tt], in_=y[:, c, :Tt])),
                         r=[("y", c)], w=[("yb", s)])
                    S.op("act", (lambda e, s=s, c=c: e.activation(out=ysq[s][:, :Tt], in_=y[:, c, :Tt], func=AF.Square)),
                         r=[("y", c)], w=[("ysq", s)])
                    S.op("pe", (lambda e, s=s, c=c: e.matmul(ps[4][:, :Tt], lhsT=self.ones[:, :], rhs=yb[s][:, :Tt],
                                                             start=(c == 0), stop=(c == KC - 1))),
                         r=[("yb", s), "ones"], w=[("ps", 4)])
                    S.op("pe", (lambda e, s=s, c=c: e.matmul(ps[5][:, :Tt], lhsT=self.ones[:, :], rhs=ysq[s][:, :Tt],
                                                             start=(c == 0), stop=(c == KC - 1))),
                         r=[("ysq", s), "ones"], w=[("ps", 5)])
                S.op("dve", (lambda e: e.tensor_scalar_mul(out=mean[:, :Tt], in0=ps[4][:, :Tt], scalar1=1.0 / D)),
                     r=[("ps", 4)], w=["mean"])
                S.op("dve", (lambda e: e.tensor_tensor(out=rs[:, :Tt], in0=mean[:, :Tt], in1=mean[:, :Tt], op=ALU.mult)),
                     r=["mean"], w=["rs"])
                S.op("dve", (lambda e: e.scalar_tensor_tensor(out=rs[:, :Tt], in0=ps[5][:, :Tt], scalar=1.0 / D,
                                                              in1=rs[:, :Tt], op0=ALU.mult, op1=ALU.subtract)),
                     r=[("ps", 5), "rs"], w=["rs"])
                S.op("dve", (lambda e: e.tensor_scalar_add(out=rs[:, :Tt], in0=rs[:, :Tt], scalar1=LN_EPS)),
                     r=["rs"], w=["rs"])
                S.op("act", (lambda e: e.activation(out=rs[:, :Tt], in_=rs[:, :Tt], func=AF.Sqrt)), r=["rs"], w=["rs"])
                S.op("dve", (lambda e: e.reciprocal(out=rs[:, :Tt], in_=rs[:, :Tt])), r=["rs"], w=["rs"])
                for c in range(KC):
                    s = c % 2
                    S.op("dve", (lambda e, s=s, c=c: e.tensor_tensor(
                        out=tn[s][:, :Tt], in0=y[:, c, :Tt], in1=mean[:, :Tt], op=ALU.subtract)),
                        r=[("y", c), "mean"], w=[("tn", s)])
                    S.op("dve", (lambda e, s=s: e.tensor_tensor(
                        out=tn[s][:, :Tt], in0=tn[s][:, :Tt], in1=rs[:, :Tt], op=ALU.mult)),
                        r=[("tn", s), "rs"], w=[("tn", s)])
                    S.op("act", (lambda e, s=s, c=c: e.activation(
                        out=z[:, c, :Tt], in_=tn[s][:, :Tt], func=AF.Silu, bias=cvs[:, c, 33:34], scale=cvs[:, c, 32:33])),
                        r=[("tn", s), "cvs"], w=[("z", c)])
                for e2 in range(8):
                    s = e2 % 2
                    S.dma("pool", w2[s][:, :, :], self.w_pw2[:, e2 * 256:(e2 + 1) * 256].rearrange("(k p) c -> p k c", p=128),
                          w=[("w2", s)])
                    for half in range(2):
                        ec = 2 * e2 + half
                        pi = 6 + ec % 2
                        xs = ec % 2
                        self.mm_fm(ps[pi], ("ps", pi), w2[s], ("w2", s), KC, z, lambda k: ("z", k), Tt, wcol0=half * 128)
                        S.dma("sp", xe[xs][:, :Tt], src[ec * 128:(ec + 1) * 128, t0:t1], w=[("xe", xs)])
                        S.op("dve", (lambda e, pi=pi, xs=xs, ec=ec: e.scalar_tensor_tensor(
                            out=xo[xs][:, :Tt], in0=ps[pi][:, :Tt], scalar=self.mod(l, 0, 2, ec), in1=xe[xs][:, :Tt],
                            op0=ALU.mult, op1=ALU.add)), r=[("ps", pi), ("M", l), ("xe", xs)], w=[("xo", xs)])
                        S.op("dve", (lambda e, xs=xs, ec=ec: e.tensor_scalar_add(
                            out=xo[xs][:, :Tt], in0=xo[xs][:, :Tt], scalar1=gb2[:, ec:ec + 1])),
                            r=[("xo", xs), "gb2"], w=[("xo", xs)])
                        S.dma("sp", dst[ec * 128:(ec + 1) * 128, t0:t1], xo[xs][:, :Tt], r=[("xo", xs)], w=[("dst", ec)])
            for (t0, t1) in _tiles(a, b, tmax):
                body(t0, t1)
            S.flush()

    def phase_final(self, src):
        S = self.S
        with ExitStack() as st:
            tmax = 1024
            nb = self.norm_bufs(st, tmax, with_vm=False)
            ps = self.ps(st, "pfin", [128, 1024])
            xo = [self.sb(st, "fo%d" % i, [128, 2, tmax], F32) for i in range(2)]
            xg = nb["xg"]
            def body(t0, t1):
                W = t1 - t0
                self.norm_stats(src, t0, t1, nb, ps, "psf")
                for g in range(8):
                    s = g % 2
                    S.dma("sp", xg[s][:, :, :W], src[g * 256:(g + 1) * 256, t0:t1].rearrange("(k p) t -> p k t", p=128),
                          w=[("xg", s)])
                    for kk in range(2):
                        k = 2 * g + kk
                        S.op("dve", (lambda e, s=s, kk=kk, k=k: e.scalar_tensor_tensor(
                            out=xo[s][:, kk, :W], in0=xg[s][:, kk, :W], scalar=self.fngs[:, k:k + 1],
                            in1=nb["rstd"][:, :W], op0=ALU.mult, op1=ALU.mult)),
                            r=[("xg", s), "rstd", "fngs"], w=[("fo", s)])
                    S.dma("sp", self.outT[g * 256:(g + 1) * 256, t0 - O0:t1 - O0].rearrange("(k p) t -> p k t", p=128),
                          xo[s][:, :, :W], r=[("fo", s)], w=[("out", g)])
            for (t0, t1) in _tiles(O0, E0, tmax):
                body(t0, t1)
            S.flush()

    def build(self):
        S, nc = self.S, self.nc
        with ExitStack() as st0:
            self.consts(st0)
            self.phase_mod()
            S.dma("sp", self.xa[:, 0:64], self.xin[:, 0:64], w=["xa_lo"])
            S.dma("sp", self.xa[:, TE - 128:TE], self.xin[:, TE - 128:TE], w=["xa_hi"])
            inval = (O0, E0)
            inval_c = (CPAD, CPAD + CTXL)
            nl = self.n_layers
            cur = None
            if nl >= 1:
                self.phase_pool(0, 0, 0, self.xin, self.xb, 63, 4737, 936, inval, self.vmask, self.rcnt,
                                bg=tuple(range(1, nl)))
                self.phase_ffn(0, 0, self.xb, self.xa, 64, 4736, 936, inval, self.vmask)
                self.phase_pool(0, 0, 1, self.cin, self.cb, 15, 273, 936, inval_c, self.vmask_c, self.rcnt_c)
                self.phase_ffn(0, 1, self.cb, self.ca, 16, 272, 936, inval_c, self.vmask_c)
            if nl >= 2:
                self.phase_qkv(0, self.xa, [(i * 1024, min((i + 1) * 1024, TE)) for i in range(5)], False)
                self.phase_qkv(1, self.ca, [(CPAD, CPAD + CTXL)], True)
                self.phase_att()
                self.phase_wo(1, QB0 * 128, (QB0 + NQB) * 128)
                self.phase_ffn(1, 0, self.xb, self.xa, 352, 4512, 936, inval, self.vmask)
            if nl >= 3:
                self.phase_convmod(2, self.xa, self.xb, 371, 4493)
                self.phase_ffn(2, 0, self.xb, self.xa, 372, 4492, 936, inval, self.vmask)
            if nl >= 4:
                self.phase_pool(3, 1, 0, self.xa, self.xb, 383, 4481, 936, inval, self.vmask, self.rcnt)
                self.phase_ffn(3, 0, self.xb, self.xa, 384, 4480, 936, inval, self.vmask)
            if self.dbg:
                S.flush()
                for name, do in self.dbg_outs.items():
                    dsrc = getattr(self, name)
                    n0 = dsrc.shape[0]
                    step = 128 if len(dsrc.shape) == 2 else 1
                    for k in range(0, n0, step):
                        S.dma("sp", do[k:k + step], dsrc[k:k + step], w=[("dbgo", name, k)])
            if nl >= 4:
                self.phase_final(self.xa)
            S.finish()
        return nc


def _chunked(v):
    return np.ascontiguousarray(v.reshape(-1, 128).T)


def _core_inputs(core, inp, shared):
    b, q = divmod(core, 4)
    g0 = q * OWN - HALO
    lo, hi = max(g0, 0), min(g0 + TE, SEQ)
    xT = np.zeros((D, TE), np.float32)
    xT[:, lo - g0:hi - g0] = inp["x"][b, lo:hi, :].T
    valid = np.zeros(TE, np.float32)
    valid[lo - g0:hi - g0] = 1.0
    cT = np.zeros((D, TEC), np.float32)
    cT[:, CPAD:CPAD + CTXL] = inp["ctx"][b].T
    cs = np.stack([_chunked(inp["c"][b]), _chunked(inp["c_ctx"])], axis=-1).astype(np.float32)
    m = dict(shared)
    m.update(xin=xT, cin=cT, vmask=np.ascontiguousarray(np.broadcast_to(valid, (128, TE))),
             rcnt=_rcnt(valid), cs=np.ascontiguousarray(cs))
    m["btab"] = shared["_btab"][0 if q == 0 else (2 if q == 3 else 1)]
    del m["_btab"]
    return m


def _rcnt(valid):
    n = valid.shape[0]
    cs = np.concatenate([[0.0], np.cumsum(valid)])
    t = np.arange(n)
    out = np.zeros((4, n), np.float32)
    for g, w in enumerate((2, 4, 8, 16)):
        lo = np.clip(t - w // 2, 0, n)
        hi = np.clip(t - w // 2 + w, 0, n)
        cnt = cs[hi] - cs[lo]
        out[g] = np.where(cnt > 0, 1.0 / np.maximum(cnt, 1.0), 0.0)
    return np.ascontiguousarray(np.broadcast_to(out, (128, 4, n))).astype(np.float32)


def _bias_tables(rpb):
    k = np.arange(128)
    kr, kc = k // 64, k % 64
    qr, qc = kr, kc
    cstart = np.clip(qc - 8, 0, GRID_W - 16)
    cval = (kc[:, None] >= cstart[None, :]) & (kc[:, None] < cstart[None, :] + 16)
    coff = np.clip(kc[:, None] - qc[None, :], -15, 15) + 15

    def tile(grow_q0, koff):
        gq = grow_q0 + qr[None, :]
        gk = grow_q0 + 2 * koff + kr[:, None]
        r0 = np.clip(gq - 4, 0, ROWS - 8)
        ok = cval & (gk >= r0) & (gk < r0 + 8) & (gk >= 0) & (gk < ROWS)
        roff = np.clip(gk - gq + 7, 0, 14)
        vals = rpb[:, roff, coff]
        return np.where(ok[None], vals, NEG).astype(np.float32)

    def tset(rowA, rowB, rowC, rowD):
        tl = []
        for off in range(-2, 3):
            tl.append(tile(100, off))
        for off in range(-2, 4):
            tl.append(tile(rowA, off))
        for off in range(-2, 3):
            tl.append(tile(rowB, off))
        for off in range(-2, 3):
            tl.append(tile(rowC, off))
        for off in range(-3, 3):
            tl.append(tile(rowD, off))
        return np.ascontiguousarray(np.stack(tl, axis=1))

    first = tset(0, 2, 100, 100)
    inter = tset(100, 100, 100, 100)
    last = tset(100, 100, ROWS - 4, ROWS - 2)
    return [first, inter, last]


def _shared_inputs(inp):
    f = np.float32
    sh = {}
    sh["w_mod"] = np.ascontiguousarray(inp["w_mod"], dtype=f)
    sh["bmod"] = np.ascontiguousarray(np.stack([_chunked(inp["b_mod"][l]) for l in range(4)]), dtype=f)
    sh["pool_w"] = np.ascontiguousarray(inp["pool_w"], dtype=f)
    sh["pool_ls"] = np.ascontiguousarray(np.stack([_chunked(inp["pool_scale"][j]) for j in range(2)]), dtype=f)
    sh["w_qkv"] = np.ascontiguousarray(inp["na_w_qkv"][0], dtype=f)
    sh["w_o"] = np.ascontiguousarray(inp["na_w_o"][0], dtype=f)
    sh["_btab"] = _bias_tables(np.asarray(inp["na_rpb"][0], dtype=f))
    sh["w_pw1"] = np.ascontiguousarray(inp["cv_w_pw1"][0], dtype=f)
    sh["w_pw2"] = np.ascontiguousarray(inp["cv_w_pw2"][0], dtype=f)
    cvv = np.zeros((128, KC, 40), f)
    for t in range(31):
        cvv[:, :, t] = _chunked(inp["cv_w_dw"][0][t])
    cvv[:, :, 31] = _chunked(inp["cv_b_dw"][0])
    cvv[:, :, 32] = _chunked(inp["cv_ln_g"][0])
    cvv[:, :, 33] = _chunked(inp["cv_ln_b"][0])
    cvv[:, :, 34] = _chunked(inp["cv_b_pw1"][0][:D])
    cvv[:, :, 35] = _chunked(inp["cv_b_pw1"][0][D:])
    cvv[:, :, 36] = _chunked(inp["cv_b_pw2"][0])
    sh["cvv"] = cvv
    sh["w_up"] = np.ascontiguousarray(inp["ffn_w_up"], dtype=f)
    sh["w_dn"] = np.ascontiguousarray(inp["ffn_w_down"], dtype=f)
    cw = np.zeros((128, 4, 2 * FC, 4), f)
    for l in range(4):
        for t in range(3):
            cw[:, l, :, t] = _chunked(inp["ffn_w_dw"][l][t])
        cw[:, l, :, 3] = _chunked(inp["ffn_b_dw"][l])
    sh["cw"] = cw
    sh["fng"] = _chunked(np.asarray(inp["final_norm_g"], dtype=f))
    sh["ident_in"] = np.eye(128, dtype=f)
    vc = np.zeros(TEC, f)
    vc[CPAD:CPAD + CTXL] = 1.0
    sh["vmask_c"] = np.ascontiguousarray(np.broadcast_to(vc, (128, TEC)))
    sh["rcnt_c"] = _rcnt(vc)
    return sh


def kernel(**inputs):
    inp = {k: np.asarray(v) for k, v in inputs.items()}
    shared = _shared_inputs(inp)
    nc = Builder().build()
    in_maps = [_core_inputs(c, inp, shared) for c in range(NCORE)]
    res = run_bass_kernel_spmd(nc, in_maps, core_ids=list(range(NCORE)))
    out = np.empty((NB, SEQ, D), np.float32)
    for c in range(NCORE):
        b, q = divmod(c, 4)
        out[b, q * OWN:(q + 1) * OWN, :] = res.results[c]["outT"].T
    return out
```
